# Optimizing a Trainium2 kernel written in Bass

```python
import math
import jax, jax.numpy as jnp
from jax import lax
import numpy as np

D_MODEL = 2048
BATCH = 8
SEQ = 2048
DEPTH = 1

ML_HEADS = 8
ML_QK_DIM = D_MODEL // 2 // ML_HEADS
ML_V_DIM = D_MODEL // ML_HEADS
ML_QK_WIDTH = ML_HEADS * ML_QK_DIM
ML_WIDTH = ML_HEADS * ML_V_DIM
GATE_SOFTCAP = 15.0
DN_HEADS = 16
DN_HEAD_DIM = D_MODEL // DN_HEADS
DN_WIDTH = DN_HEADS * DN_HEAD_DIM
CONV_WIDTH = 4
CHUNK = 64
NORM_EPS = 1e-6

IN_SIZES = (ML_QK_WIDTH, ML_QK_WIDTH, ML_WIDTH, ML_WIDTH, ML_WIDTH, ML_HEADS, ML_HEADS,
            3 * DN_WIDTH, DN_WIDTH, DN_HEADS, DN_HEADS,
            D_MODEL, D_MODEL)
IN_DIM = sum(IN_SIZES)
OUT_DIM = ML_WIDTH + DN_WIDTH

kernel_name = "hybrid_mlstm_gdn_gated_merge"


def _rmsnorm(x, w=None):
    xf = x.astype(jnp.float32)
    y = xf * lax.rsqrt(jnp.mean(xf * xf, axis=-1, keepdims=True) + NORM_EPS)
    return y if w is None else y * w.astype(jnp.float32)


def _l2norm(x):
    return x * lax.rsqrt(jnp.sum(x * x, axis=-1, keepdims=True) + NORM_EPS)


def _heads(t, n):
    b, s, _ = t.shape
    return t.reshape(b, s, n, -1).transpose(0, 2, 1, 3)


def _merge_heads(t):
    b, h, s, d = t.shape
    return t.transpose(0, 2, 1, 3).reshape(b, s, h * d)


def _chunks(t):
    b, h, s = t.shape[:3]
    t = t.reshape(b, h, s // CHUNK, CHUNK, *t.shape[3:])
    return jnp.moveaxis(t, 2, 0)


def _unchunk(t):
    nc, b, h, l, d = t.shape
    return jnp.moveaxis(t, 0, 2).reshape(b, h, nc * l, d)


def _softcap(t):
    return GATE_SOFTCAP * jnp.tanh(t / GATE_SOFTCAP)


def _mlstm_chunkwise(q, k, v, i_pre, logf):
    b_, h_, s_, dk = q.shape
    dv = v.shape[-1]
    q = q * (dk ** -0.5)
    qc, kc, vc = _chunks(q), _chunks(k), _chunks(v)
    ic, fc = _chunks(i_pre), _chunks(logf)
    bcum = jnp.cumsum(fc, axis=-1)
    causal = jnp.tril(jnp.ones((CHUNK, CHUNK), dtype=bool))
    dmat = bcum[..., :, None] - bcum[..., None, :] + ic[..., None, :]
    dmat = jnp.where(causal, dmat, -jnp.inf)
    dmax = jnp.max(dmat, axis=-1)
    scores = jnp.einsum('cbhtd,cbhsd->cbhts', qc, kc)
    w_end = bcum[..., -1:] - bcum + ic
    w_end_max = jnp.max(w_end, axis=-1)

    def step(carry, inp):
        C, n, m = carry
        q_c, k_c, v_c, b_c, dm_c, dmax_c, sc_c, we_c, wem_c = inp
        inter_log = b_c + m[..., None]
        m_t = jnp.maximum(inter_log, dmax_c)
        inter_w = jnp.exp(inter_log - m_t)
        intra_w = jnp.exp(dm_c - m_t[..., None]) * sc_c
        num = (inter_w[..., None] * jnp.einsum('bhtd,bhde->bhte', q_c, C)
               + jnp.einsum('bhts,bhse->bhte', intra_w, v_c))
        den = inter_w * jnp.einsum('bhtd,bhd->bht', q_c, n) + jnp.sum(intra_w, axis=-1)
        h = num / jnp.maximum(jnp.abs(den), jnp.exp(-m_t))[..., None]
        m_new = jnp.maximum(b_c[..., -1] + m, wem_c)
        decay = jnp.exp(b_c[..., -1] + m - m_new)
        kw = k_c * jnp.exp(we_c - m_new[..., None])[..., None]
        C_new = decay[..., None, None] * C + jnp.einsum('bhsd,bhse->bhde', kw, v_c)
        n_new = decay[..., None] * n + jnp.sum(kw, axis=-2)
        return (C_new, n_new, m_new), h

    init = (jnp.zeros((b_, h_, dk, dv), jnp.float32),
            jnp.zeros((b_, h_, dk), jnp.float32),
            jnp.zeros((b_, h_), jnp.float32))
    _, hs = lax.scan(step, init, (qc, kc, vc, bcum, dmat, dmax, scores, w_end, w_end_max))
    return _unchunk(hs)


def _gated_delta_chunkwise(q, k, v, beta, g):
    b_, h_, s_, dk = q.shape
    dv = v.shape[-1]
    q = q * (dk ** -0.5)
    qc, kc, vc = _chunks(q), _chunks(k), _chunks(v)
    bc, gc = _chunks(beta), _chunks(g)
    gam = jnp.cumsum(gc, axis=-1)
    incl = jnp.tril(jnp.ones((CHUNK, CHUNK), dtype=bool))
    strict = jnp.tril(jnp.ones((CHUNK, CHUNK), dtype=bool), k=-1)
    rel = gam[..., :, None] - gam[..., None, :]
    decay_mat = jnp.where(incl, jnp.exp(jnp.where(incl, rel, 0.0)), 0.0)
    kk = jnp.einsum('cbhtd,cbhsd->cbhts', kc, kc)
    a_mat = jnp.where(strict, bc[..., :, None] * kk * decay_mat, 0.0)
    lhs = a_mat + jnp.eye(CHUNK, dtype=a_mat.dtype)
    rhs = jnp.concatenate([vc * bc[..., None], kc * (bc * jnp.exp(gam))[..., None]], axis=-1)
    sol = lax.linalg.triangular_solve(lhs, rhs, left_side=True, lower=True, unit_diagonal=True)
    w_val, w_k = sol[..., :dv], sol[..., dv:]
    qk = jnp.einsum('cbhtd,cbhsd->cbhts', qc, kc) * decay_mat
    q_dec = qc * jnp.exp(gam)[..., None]
    k_end = kc * jnp.exp(gam[..., -1:] - gam)[..., None]
    g_end = jnp.exp(gam[..., -1])

    def step(S, inp):
        wv, wk, qk_c, qd, ke, ge = inp
        u = wv - jnp.einsum('bhtd,bhde->bhte', wk, S)
        o = jnp.einsum('bhtd,bhde->bhte', qd, S) + jnp.einsum('bhts,bhse->bhte', qk_c, u)
        S_new = ge[..., None, None] * S + jnp.einsum('bhsd,bhse->bhde', ke, u)
        return S_new, o

    S0 = jnp.zeros((b_, h_, dk, dv), jnp.float32)
    _, os_ = lax.scan(step, S0, (w_val, w_k, qk, q_dec, k_end, g_end))
    return _unchunk(os_)


def _causal_conv_silu(x, w):
    kw = w.shape[0]
    s = x.shape[1]
    xp = jnp.pad(x, ((0, 0), (kw - 1, 0), (0, 0)))
    y = sum(w[j].astype(jnp.float32) * xp[:, j:j + s] for j in range(kw))
    return jax.nn.silu(y)


def _hybrid_layer(h, norm_w, w_in, ml_i_bias, ml_f_bias, ml_norm_w,
                  dn_conv_w, dn_a_log, dn_dt_bias, dn_norm_w, w_out):
    u = _rmsnorm(h, norm_w)
    proj = jnp.matmul(u, w_in.astype(jnp.float32))
    idx = np.cumsum(IN_SIZES)[:-1].tolist()
    (ml_q, ml_k, ml_v, ml_o, ml_z, ml_i, ml_f,
     dn_qkv, dn_z, dn_b, dn_a, gate_a, gate_b) = jnp.split(proj, idx, axis=-1)

    i_pre = _softcap(ml_i + ml_i_bias).transpose(0, 2, 1)
    logf = jax.nn.log_sigmoid(_softcap(ml_f + ml_f_bias)).transpose(0, 2, 1)
    h_a = _mlstm_chunkwise(_heads(ml_q, ML_HEADS), _heads(ml_k, ML_HEADS),
                           _heads(ml_v, ML_HEADS), i_pre, logf)
    h_a = _merge_heads(_rmsnorm(h_a)) * ml_norm_w.astype(jnp.float32)
    h_a = h_a * jax.nn.sigmoid(ml_o) * jax.nn.silu(ml_z)

    qkv = _causal_conv_silu(dn_qkv, dn_conv_w)
    dq, dk_, dv_ = jnp.split(qkv, 3, axis=-1)
    dq = _l2norm(_heads(dq, DN_HEADS))
    dk_ = _l2norm(_heads(dk_, DN_HEADS))
    dv_ = _heads(dv_, DN_HEADS)
    beta = jax.nn.sigmoid(dn_b).transpose(0, 2, 1)
    g = (-jnp.exp(dn_a_log.astype(jnp.float32))
         * jax.nn.softplus(dn_a + dn_dt_bias)).transpose(0, 2, 1)
    h_b = _gated_delta_chunkwise(dq, dk_, dv_, beta, g)
    h_b = _rmsnorm(h_b, dn_norm_w)
    h_b = _merge_heads(h_b) * jax.nn.silu(dn_z)

    w_out = w_out.astype(jnp.float32)
    y_a = jnp.matmul(h_a, w_out[:ML_WIDTH])
    y_b = jnp.matmul(h_b, w_out[ML_WIDTH:])
    return jax.nn.sigmoid(gate_a) * y_a + jax.nn.sigmoid(gate_b) * y_b


def setup_inputs(seed: int = 0) -> dict:
    key = jax.random.key(seed)
    ks = jax.random.split(key, 13)
    x = jax.random.normal(ks[0], (BATCH, SEQ, D_MODEL), jnp.float32)
    norm_w = 1.0 + 0.05 * jax.random.normal(ks[1], (DEPTH, D_MODEL), jnp.float32)
    w_in = jax.random.normal(ks[2], (DEPTH, D_MODEL, IN_DIM), jnp.float32) * D_MODEL ** -0.5
    ml_i_bias = 0.1 * jax.random.normal(ks[3], (DEPTH, ML_HEADS), jnp.float32)
    ml_f_bias = 3.0 + 0.5 * jax.random.normal(ks[4], (DEPTH, ML_HEADS), jnp.float32)
    ml_norm_w = 1.0 + 0.05 * jax.random.normal(ks[5], (DEPTH, ML_WIDTH), jnp.float32)
    dn_conv_w = jax.random.normal(ks[6], (DEPTH, CONV_WIDTH, 3 * DN_WIDTH), jnp.float32) * CONV_WIDTH ** -0.5
    dn_a_log = jnp.log(jax.random.uniform(ks[7], (DEPTH, DN_HEADS), jnp.float32, 1.0, 16.0))
    dt = jnp.exp(jax.random.uniform(ks[8], (DEPTH, DN_HEADS), jnp.float32,
                                    math.log(1e-3), math.log(1e-1)))
    dn_dt_bias = dt + jnp.log(-jnp.expm1(-dt))
    dn_norm_w = 1.0 + 0.05 * jax.random.normal(ks[9], (DEPTH, DN_HEAD_DIM), jnp.float32)
    w_out = jax.random.normal(ks[10], (DEPTH, OUT_DIM, D_MODEL), jnp.float32) * OUT_DIM ** -0.5
    final_norm_w = 1.0 + 0.05 * jax.random.normal(ks[11], (D_MODEL,), jnp.float32)
    return {"x": x, "norm_w": norm_w, "w_in": w_in, "ml_i_bias": ml_i_bias,
            "ml_f_bias": ml_f_bias, "ml_norm_w": ml_norm_w, "dn_conv_w": dn_conv_w,
            "dn_a_log": dn_a_log, "dn_dt_bias": dn_dt_bias, "dn_norm_w": dn_norm_w,
            "w_out": w_out, "final_norm_w": final_norm_w}


def reference(x, norm_w, w_in, ml_i_bias, ml_f_bias, ml_norm_w, dn_conv_w,
              dn_a_log, dn_dt_bias, dn_norm_w, w_out, final_norm_w):
    h = x
    for l in range(DEPTH):
        y = _hybrid_layer(h, norm_w[l], w_in[l], ml_i_bias[l], ml_f_bias[l], ml_norm_w[l],
                          dn_conv_w[l], dn_a_log[l], dn_dt_bias[l], dn_norm_w[l], w_out[l])
        h = h + y.astype(h.dtype)
    return _rmsnorm(h, final_norm_w).astype(x.dtype)
```

```python
import contextlib
import numpy as np
import concourse.bass as bass
import concourse.mybir as mybir
from concourse.bass_utils import run_bass_kernel_spmd

F32 = mybir.dt.float32
BF16 = mybir.dt.bfloat16
F32R = mybir.dt.float32r
AF = mybir.ActivationFunctionType
ALU = mybir.AluOpType
AX = mybir.AxisListType

D_MODEL = 2048
SEQ = 2048
IN_DIM = 20528
NG = 4
TG = 512
NT = 4
EPS = 1e-6
OFF_MLQ, OFF_MLK, OFF_MLV, OFF_MLO, OFF_MLZ = 0, 1024, 2048, 4096, 6144
OFF_MLI = 8192
OFF_DNQ, OFF_DNK, OFF_DNV = 8208, 8208 + 2048, 8208 + 4096
OFF_DNZ = 14352
OFF_DNB = 16400
OFF_GA, OFF_GB = 16432, 18480

ENGS = ("pe", "act", "dve", "pool", "sp")


class Buf:
    __slots__ = ("writers", "readers")

    def __init__(self):
        self.writers = {}
        self.readers = {}


class T:
    __slots__ = ("ap", "buf")

    def __init__(self, ap, buf=None):
        self.ap = ap
        self.buf = buf if buf is not None else Buf()

    def __getitem__(self, key):
        return T(self.ap[key], self.buf)

    def bc(self, shape):
        return T(self.ap.to_broadcast(list(shape)), self.buf)

    def unsq(self, axis):
        return T(self.ap.unsqueeze(axis), self.buf)

    def re(self, pattern, **kw):
        return T(self.ap.rearrange(pattern, **kw), self.buf)

    def bitcast(self, dt):
        return T(self.ap.bitcast(dt), self.buf)


def _ap(x):
    return x.ap if isinstance(x, T) else x


def _bufs(*xs):
    return [x.buf for x in xs if isinstance(x, T)]


class Op:
    __slots__ = ("eng", "idx", "fn", "waits", "signal", "is_dma", "dma_sem", "dma_val", "dma_prev")

    def __init__(self, eng, idx, fn, is_dma):
        self.eng = eng
        self.idx = idx
        self.fn = fn
        self.waits = []
        self.signal = False
        self.is_dma = is_dma
        self.dma_sem = None
        self.dma_val = None
        self.dma_prev = None


class Sched:
    def __init__(self, nc, n_dma_sems=32):
        self.nc = nc
        self.ops = {e: [] for e in ENGS}
        self.seen = {e: {} for e in ENGS}
        self.n_dma_sems = n_dma_sems
        self.dma_rr = 0
        self.dma_last = [None] * n_dma_sems
        self.dma_count = [0] * n_dma_sems
        self.out_dmas = []

    def _need(self, op, key, val):
        e = op.eng
        if isinstance(key, str):
            if key == e and e == "pe":
                return
            if self.seen[e].get(key, -1) >= val:
                return
            self.seen[e][key] = val
            self.ops[key][val].signal = True
            op.waits.append(("eng", key, val))
        else:
            if key in self.seen[e]:
                return
            self.seen[e][key] = 1
            op.waits.append(("dma", val))

    def op(self, eng, fn, reads=(), writes=(), dma=False):
        o = Op(eng, len(self.ops[eng]), fn, dma)
        for b in reads:
            for k, v in b.writers.items():
                self._need(o, k, v)
        for b in writes:
            for k, v in b.writers.items():
                self._need(o, k, v)
            for k, v in b.readers.items():
                if k == eng and not dma:
                    continue
                self._need(o, k, v)
        self.ops[eng].append(o)
        if dma:
            k = self.dma_rr
            self.dma_rr = (k + 1) % self.n_dma_sems
            o.dma_sem = k
            o.dma_prev = self.dma_last[k]
            self.dma_count[k] += 1
            o.dma_val = 16 * self.dma_count[k]
            self.dma_last[k] = o
            key = ("dma", id(o))
            for b in reads:
                b.readers[key] = o
            for b in writes:
                b.writers = {key: o}
                b.readers = {}
        else:
            for b in reads:
                b.readers[eng] = o.idx
            for b in writes:
                b.writers = {eng: o.idx}
                b.readers = {}
        return o

    def dma(self, q, out, in_):
        return self.op(q, lambda e, o=_ap(out), i=_ap(in_): e.dma_start(out=o, in_=i),
                       _bufs(in_), _bufs(out), dma=True)

    def mm(self, out, lhsT, rhs, start=True, stop=True):
        return self.op("pe", lambda e, o=_ap(out), l=_ap(lhsT), r=_ap(rhs): e.matmul(
            o, lhsT=l, rhs=r, start=start, stop=stop), _bufs(lhsT, rhs), _bufs(out))

    def tr(self, out, in_, ident):
        return self.op("pe", lambda e, o=_ap(out), i=_ap(in_), d=_ap(ident): e.transpose(o, i, d),
                       _bufs(in_, ident), _bufs(out))

    def act(self, out, in_, func, bias=0.0, scale=1.0, accum_out=None, eng="act"):
        kw = {}
        if accum_out is not None:
            kw["accum_out"] = _ap(accum_out)
        return self.op("act", lambda e, o=_ap(out), i=_ap(in_), b=_ap(bias), s=_ap(scale): e.activation(
            o, i, func, bias=b, scale=s, **kw), _bufs(in_, bias, scale), _bufs(out, accum_out))

    def tt(self, eng, out, in0, in1, op):
        return self.op(eng, lambda e, o=_ap(out), a=_ap(in0), b=_ap(in1): e.tensor_tensor(o, a, b, op),
                       _bufs(in0, in1), _bufs(out))

    def ts(self, eng, out, in0, s1, s2, op0, op1=None):
        if op1 is None and isinstance(s1, T) and eng == "dve":
            s2, op1 = 0.0, ALU.add
        if op1 is None:
            return self.op(eng, lambda e, o=_ap(out), a=_ap(in0), x=_ap(s1): e.tensor_scalar(
                o, a, x, None, op0), _bufs(in0, s1), _bufs(out))
        return self.op(eng, lambda e, o=_ap(out), a=_ap(in0), x=_ap(s1), y=_ap(s2): e.tensor_scalar(
            o, a, x, y, op0, op1), _bufs(in0, s1, s2), _bufs(out))

    def stt(self, out, in0, scalar, in1, op0, op1):
        return self.op("dve", lambda e, o=_ap(out), a=_ap(in0), x=_ap(scalar), b=_ap(in1): e.scalar_tensor_tensor(
            o, a, x, b, op0, op1), _bufs(in0, scalar, in1), _bufs(out))

    def cp(self, eng, out, in_):
        if eng == "act":
            return self.act(out, in_, AF.Copy)
        return self.op(eng, lambda e, o=_ap(out), i=_ap(in_): e.tensor_copy(o, i), _bufs(in_), _bufs(out))

    def memset(self, eng, out, val):
        return self.op(eng, lambda e, o=_ap(out): e.memset(o, val), (), _bufs(out))

    def recip(self, out, in_):
        return self.op("dve", lambda e, o=_ap(out), i=_ap(in_): e.reciprocal(o, i), _bufs(in_), _bufs(out))

    def reduce_sum(self, out, in_):
        return self.op("dve", lambda e, o=_ap(out), i=_ap(in_): e.reduce_sum(o, i, AX.X), _bufs(in_), _bufs(out))

    def emit(self):
        nc = self.nc
        with contextlib.ExitStack() as st:
            esem = {e: st.enter_context(nc.semaphore("s_" + e)) for e in ENGS}
            dsem = [st.enter_context(nc.semaphore("d%d" % i)) for i in range(self.n_dma_sems)]
            block = st.enter_context(nc.Block())
            signum = {}
            for e in ENGS:
                c = 0
                for o in self.ops[e]:
                    if o.signal and not o.is_dma:
                        c += 1
                    signum[(e, o.idx)] = c

            def run(e):
                def body(engine):
                    for o in self.ops[e]:
                        for w in o.waits:
                            if w[0] == "eng":
                                engine.wait_ge(esem[w[1]], signum[(w[1], w[2])])
                            else:
                                d = w[1]
                                engine.wait_ge(dsem[d.dma_sem], d.dma_val)
                        if o.is_dma and o.dma_prev is not None:
                            engine.wait_ge(dsem[o.dma_sem], o.dma_prev.dma_val)
                        ins = o.fn(engine)
                        if o.is_dma:
                            ins.then_inc(dsem[o.dma_sem], 16)
                        elif o.signal:
                            ins.then_inc(esem[e], 1)
                    if e == "sp":
                        for d in self.out_dmas:
                            engine.wait_ge(dsem[d.dma_sem], d.dma_val)
                return body

            block.tensor(run("pe"))
            block.scalar(run("act"))
            block.vector(run("dve"))
            block.gpsimd(run("pool"))
            block.sync(run("sp"))


class Arena:
    def __init__(self, nc, words):
        self.t = nc.alloc_sbuf_tensor("arena", [128, words], F32)
        self.ap = self.t.ap()
        self.top = 0
        self.words = words
        self.hist = []

    def alloc(self, shape, dt):
        n = int(np.prod(shape))
        bpe = 2 if dt == BF16 else 4
        w = (n * bpe + 3) // 4
        w = (w + 7) // 8 * 8
        assert self.top + w <= self.words, ("arena overflow", self.top, w, self.words)
        a = self.ap[:, self.top:self.top + w]
        nb = Buf()
        lo, hi = self.top, self.top + w
        keep = []
        for (s0, e0, b0) in self.hist:
            if s0 < hi and lo < e0:
                for src, dst in ((b0.writers, nb.writers), (b0.readers, nb.readers)):
                    for k, v in src.items():
                        if isinstance(k, str):
                            dst[k] = max(dst.get(k, -1), v)
                        else:
                            dst[k] = v
                if s0 < lo or e0 > hi:
                    keep.append((s0, e0, b0))
            else:
                keep.append((s0, e0, b0))
        keep.append((lo, hi, nb))
        self.hist = keep
        self.top += w
        if dt != F32:
            a = a.bitcast(dt)
        a = a[:, 0:n]
        if len(shape) == 2:
            a = a.rearrange("p (a b) -> p a b", b=shape[1])
        elif len(shape) == 3:
            a = a.rearrange("p (a b c) -> p a b c", b=shape[1], c=shape[2])
        return T(a, nb)

    def mark(self):
        return self.top

    def release(self, m):
        self.top = m


C_IDENT, C_TRI, C_ONES, C_NEGONES, C_TRIDN, C_BLK, C_H0, C_H1, C_NEGMASK, C_STRICT = range(10)
NCONST = 10
P_NW, P_FNW, P_MLNW, P_DNNW, P_GBIAS, P_ALOG, P_CONV = 0, 2048, 4096, 4112, 4240, 4288, 4304
NPARAM = 4304 + 192


def host_consts():
    s = np.arange(128)[:, None]
    t = np.arange(128)[None, :]
    same = (s // 64) == (t // 64)
    m = np.zeros((NCONST, 128, 128), np.float32)
    m[C_IDENT] = (s == t)
    m[C_TRI] = (s <= t)
    m[C_ONES] = 1.0
    m[C_NEGONES] = -1.0
    m[C_TRIDN] = (s <= t) & same
    m[C_BLK] = same
    m[C_H0] = (s < 64) & (t >= 0)
    m[C_H1] = (s >= 64) & (t >= 0)
    m[C_NEGMASK] = np.where((s <= t) & same, 0.0, -30000.0)
    m[C_STRICT] = (s < t) & same
    return np.ascontiguousarray(m.transpose(1, 0, 2).reshape(128, NCONST * 128))


def build(debug=None):
    nc = bass.Bass("TRN2", target_bir_lowering=False)
    x_d = nc.dram_tensor("x", [SEQ, D_MODEL], F32, kind="ExternalInput").ap()
    win_d = nc.dram_tensor("w_in", [D_MODEL, IN_DIM], F32, kind="ExternalInput").ap()
    wout_d = nc.dram_tensor("w_out", [4096, D_MODEL], F32, kind="ExternalInput").ap()
    par_d = nc.dram_tensor("params", [128, NPARAM], F32, kind="ExternalInput").ap()
    con_d = nc.dram_tensor("consts", [128, NCONST * 128], F32, kind="ExternalInput").ap()
    out_d = nc.dram_tensor("out", [SEQ, D_MODEL], F32, kind="ExternalOutput").ap()
    dbg_d = {}
    if debug:
        for name, shape in debug.items():
            if name.startswith("_"):
                continue
            dbg_d[name] = nc.dram_tensor("dbg_" + name, [128] + list(shape), F32, kind="ExternalOutput").ap()

    winv = win_d.rearrange("(kc p) n -> p kc n", p=128)
    woutv = wout_d.rearrange("(jc p) n -> p jc n", p=128)
    xg = T(x_d)
    outg = T(out_d)

    s = Sched(nc)
    A = Arena(nc, 53100)
    banks = [T(nc.alloc_psum_tensor("bank%d" % i, [128, 512], F32).ap()) for i in range(8)]

    consts = A.alloc([NCONST, 128], F32)
    cm = [consts[:, i, :] for i in range(NCONST)]
    ident_bf = A.alloc([128], BF16)
    ones_bf = A.alloc([128], BF16)
    par = A.alloc([NPARAM - 4096], F32)
    PO = 4096

    def pslice(off, n):
        return par[:, off - PO:off - PO + n]
    mlnwT = pslice(P_MLNW, 16)
    dnnw = pslice(P_DNNW, 128)
    gbias = pslice(P_GBIAS, 48)
    alog = pslice(P_ALOG, 16)
    convw = T(par.ap[:, P_CONV - PO:P_CONV - PO + 192].rearrange("p (a b) -> p a b", b=4), par.buf)
    expA = A.alloc([16], F32)
    xT = A.alloc([16, TG], BF16)
    wg = A.alloc([16, 48], BF16)
    NRING = 3
    ring = [A.alloc([16, 512], BF16) for _ in range(NRING)]
    hgAT = A.alloc([16, TG], BF16)
    hgBT = A.alloc([16, TG], BF16)
    Cst = A.alloc([8, 258], F32)
    Cbf = A.alloc([8, 258], BF16)
    Sst = A.alloc([16, 128], F32)
    Sbf = A.alloc([16, 128], BF16)
    carry = A.alloc([48, 4], F32)
    graw = A.alloc([NT, 48], F32)
    gth = A.alloc([NT, 16], F32)
    glf = A.alloc([NT, 8], F32)
    gbeta = A.alloc([NT, 16], F32)
    gnegbeta = A.alloc([NT, 16], F32)
    gneg = A.alloc([NT, 16], F32)
    gsum = A.alloc([NT, 80], F32)
    gtmp = A.alloc([NT, 16], F32)
    g_eib = A.alloc([NT, 8], F32)
    g_eb = A.alloc([NT, 8], F32)
    g_kwsc = A.alloc([NT, 8], F32)
    g_ebl = A.alloc([NT, 8], F32)
    g_egam = A.alloc([NT, 16], F32)
    g_kesc = A.alloc([NT, 16], F32)
    g_egl = A.alloc([NT, 32], F32)
    small = A.alloc([64], F32)
    base_mark = A.mark()

    s.dma("sp", consts, T(con_d))
    s.dma("sp", par, T(par_d[:, PO:NPARAM]))
    s.cp("dve", ident_bf, cm[C_IDENT])
    s.cp("dve", ones_bf, cm[C_ONES])
    s.act(expA, alog, AF.Exp)
    s.memset("pool", Cst, 0.0)
    s.memset("pool", Cbf, 0.0)
    s.memset("pool", Sst, 0.0)
    s.memset("pool", Sbf, 0.0)
    s.memset("pool", carry, 0.0)
    s.memset("pool", small, -0.5)
    s.dma("pool", wg[:, :, 0:16], T(winv[:, :, OFF_MLI:OFF_MLI + 16]))
    s.dma("pool", wg[:, :, 16:48], T(winv[:, :, OFF_DNB:OFF_DNB + 32]))

    ring_i = [0]

    def load_w(src_list):
        slot = ring[ring_i[0] % NRING]
        ring_i[0] += 1
        for off, src in src_list:
            n = src.shape[2]
            k = src.shape[1]
            s.dma("pool", slot[:, 0:k, off:off + n], T(src))
        return slot

    plan = []
    plan_loaded = []

    def get_w():
        while len(plan_loaded) < NRING and plan:
            plan_loaded.append(load_w(plan.pop(0)))
        return plan_loaded.pop(0)

    acc_i = [0]

    def next_acc():
        b = banks[acc_i[0] % 2]
        acc_i[0] += 1
        return b

    ev_i = [0]

    def ev_eng():
        ev_i[0] += 1
        return "act" if ev_i[0] % 2 else "dve"

    def proj_tok(w, j, ncols=512, src=None, nk=16):
        src = xT if src is None else src
        b = next_acc()
        for kc in range(nk):
            s.mm(b[:, 0:ncols], src[:, kc, j * 128:(j + 1) * 128], w[:, kc, 0:ncols],
                 start=(kc == 0), stop=(kc == nk - 1))
        return b

    def proj_feat(w, c0):
        b = next_acc()
        for kc in range(16):
            s.mm(b, w[:, kc, c0:c0 + 128], xT[:, kc, :], start=(kc == 0), stop=(kc == 15))
        return b

    def rsqrt_(out, in_, scale, tmp, eps=EPS):
        s.ts("dve", tmp, in_, scale, eps, ALU.mult, ALU.add)
        mh = small[:, 1:2]
        shp = list(tmp.ap.shape)
        for ax in range(2, len(shp)):
            mh = mh.unsq(ax)
        s.tt("pool", out, tmp, mh.bc(shp), ALU.pow)

    def dump(name, t):
        if debug and name in debug:
            s.out_dmas.append(s.dma("pool", T(dbg_d[name]), t))

    class _Stop(Exception):
        pass

    ML_RATIO = (3, 1)
    DN_RATIO = (1, 1)

    def run_all(gen):
        for _ in gen:
            pass

    def interleave3(*gens):
        live = [g_ for g_ in gens if g_ is not None]
        while live:
            for g_ in list(live):
                try:
                    next(g_)
                except StopIteration:
                    live.remove(g_)

    def interleave(ga, gb, ratio):
        na, nb_ = ratio
        da = ga is None
        db = gb is None
        while not (da and db):
            for _ in range(na):
                if not da:
                    try:
                        next(ga)
                    except StopIteration:
                        da = True
            for _ in range(nb_):
                if not db:
                    try:
                        next(gb)
                    except StopIteration:
                        db = True

    def stop_at(name):
        if debug and debug.get("_stop") == name:
            raise _Stop()

    def body(g):
        t0 = g * TG
        A.release(base_mark)
        vecbuf = A.alloc([2048], F32)
        s.dma("sp", vecbuf, T(par_d[:, P_NW:P_NW + 2048]))
        xs = [A.alloc([2048], F32) for _ in range(2)]
        junk = A.alloc([2048], BF16)
        ssx = A.alloc([NT, 2], F32)
        rsx = A.alloc([NT, 2], F32)
        tmpx = A.alloc([NT, 2], F32)
        s.memset("dve", ssx, 1.0)
        for j in range(NT):
            xt = xs[j % 2]
            s.dma("sp", xt, xg[t0 + j * 128:t0 + (j + 1) * 128, :])
            s.act(junk, xt, AF.Square, accum_out=ssx[:, j, 0:1])
            rsqrt_(rsx[:, j, :], ssx[:, j, :], 1.0 / D_MODEL, tmpx[:, j, :])
            s.stt(xt, xt, rsx[:, j, 0:1], vecbuf, ALU.mult, ALU.mult)
            for q in range(4):
                bk = banks[4 + q]
                for c in range(4):
                    kc = q * 4 + c
                    s.tr(bk[:, c * 128:(c + 1) * 128], xt[:, kc * 128:(kc + 1) * 128], cm[C_IDENT])
                s.cp(ev_eng(), xT[:, q * 4:(q + 1) * 4, j * 128:(j + 1) * 128],
                     bk.re("p (a b) -> p a b", b=128))
        if g == 0:
            dump("xT", xT)
        stop_at("phaseA")

        for j in range(NT):
            b = proj_tok(wg, j, 48)
            s.cp("dve", graw[:, j, :], b[:, 0:48])
        s.tt("dve", graw, graw, gbias.unsq(1).bc([128, NT, 48]), ALU.add)
        s.act(gth, graw[:, :, 0:16], AF.Tanh, scale=1.0 / 15.0)
        s.act(glf, gth[:, :, 8:16], AF.Exp, scale=-15.0)
        s.act(gneg, graw[:, :, 32:48], AF.Exp)
        s.act(glf, glf, AF.Ln, bias=1.0)
        s.act(gneg, gneg, AF.Ln, bias=1.0)
        s.tt("dve", gneg, gneg, expA.unsq(1).bc([128, NT, 16]), ALU.mult)
        s.act(gbeta, graw[:, :, 16:32], AF.Tanh, scale=0.5)
        s.ts("dve", gbeta, gbeta, 0.5, 0.5, ALU.mult, ALU.add)
        s.ts("dve", gnegbeta, gbeta, -1.0, None, ALU.mult)
        bsum = banks[7]
        for j in range(NT):
            o = j * 80
            s.mm(bsum[:, o:o + 8], cm[C_TRI], glf[:, j, :])
            s.mm(bsum[:, o + 8:o + 16], cm[C_ONES], glf[:, j, :])
            s.mm(bsum[:, o + 16:o + 32], cm[C_TRIDN], gneg[:, j, :])
            s.mm(bsum[:, o + 32:o + 48], cm[C_BLK], gneg[:, j, :])
            s.mm(bsum[:, o + 48:o + 64], cm[C_H0], gneg[:, j, :])
            s.mm(bsum[:, o + 64:o + 80], cm[C_H1], gneg[:, j, :])
        s.cp("dve", gsum, bsum[:, 0:NT * 80].re("p (a b) -> p a b", b=80))
        nb = gsum[:, :, 0:8]
        nbt = gsum[:, :, 8:16]
        ngam = gsum[:, :, 16:32]
        ngblk = gsum[:, :, 32:48]
        s.stt(gtmp[:, :, 0:8], gth[:, :, 0:8], 15.0, nb, ALU.mult, ALU.add)
        s.act(g_eib, gtmp[:, :, 0:8], AF.Exp)
        s.tt("dve", gtmp[:, :, 0:8], gtmp[:, :, 0:8], nbt, ALU.subtract)
        s.act(g_kwsc, gtmp[:, :, 0:8], AF.Exp)
        s.act(g_eb, nb, AF.Exp, scale=-1.0)
        s.act(g_ebl, nbt, AF.Exp, scale=-1.0)
        s.act(g_egam, ngam, AF.Exp, scale=-1.0)
        s.tt("dve", gtmp, ngam, ngblk, ALU.subtract)
        s.act(g_kesc, gtmp, AF.Exp)
        s.act(g_egl, gsum[:, :, 48:80], AF.Exp, scale=-1.0)
        if g == 0:
            dump("gsum", gsum)
            dump("eib", g_eib)
            dump("beta", gbeta)
        stop_at("gates")

        gmark = base_mark
        A.release(gmark)
        qkT2 = [A.alloc([4, TG], BF16) for _ in range(2)]
        vext2 = [A.alloc([NT, 2, 258], BF16) for _ in range(2)]
        og2 = [A.alloc([NT, 512], F32) for _ in range(2)]
        ztmp = A.alloc([512], F32)
        numbuf = A.alloc([NT, 2, 258], F32)
        hn = A.alloc([NT, 2, 256], F32)
        sqt = hn2 = A.alloc([NT, 2, 256], F32)
        hgb = A.alloc([NT, 512], BF16)
        kw = [A.alloc([128], BF16) for _ in range(2)]
        pT = [A.alloc([128], BF16) for _ in range(2)]
        nden = A.alloc([NT, 2], F32)
        nr = A.alloc([NT, 2], F32)
        nss = A.alloc([NT, 2], F32)
        ntmp = A.alloc([NT, 2], F32)
        s.memset("pool", vext2[0], 1.0)
        s.memset("pool", vext2[1], 1.0)

        for hp_ in range(4):
            plan.append([(0, winv[:, :, OFF_MLQ + 256 * hp_:OFF_MLQ + 256 * hp_ + 256]),
                         (256, winv[:, :, OFF_MLK + 256 * hp_:OFF_MLK + 256 * hp_ + 256])])
            plan.append([(0, winv[:, :, OFF_MLV + 512 * hp_:OFF_MLV + 512 * hp_ + 512])])
            plan.append([(0, winv[:, :, OFF_MLO + 512 * hp_:OFF_MLO + 512 * hp_ + 512])])
            plan.append([(0, winv[:, :, OFF_MLZ + 512 * hp_:OFF_MLZ + 512 * hp_ + 512])])

        def ml_proj(hp):
            qkT, vext, og = qkT2[hp % 2], vext2[hp % 2], og2[hp % 2]
            wqk = get_w()
            for c4 in range(4):
                b = proj_feat(wqk, c4 * 128)
                if c4 < 2:
                    s.act(qkT[:, c4, :], b, AF.Copy, scale=128.0 ** -0.5)
                else:
                    s.cp("dve", qkT[:, c4, :], b)
                yield
            wv = get_w()
            for j in range(NT):
                b = proj_tok(wv, j)
                s.cp(ev_eng(), vext[:, j, :, 0:256], b.re("p (a b) -> p a b", b=256))
                yield
            wo = get_w()
            for j in range(NT):
                b = proj_tok(wo, j)
                s.act(og[:, j, :], b, AF.Tanh, scale=0.5)
                yield
            wz = get_w()
            for j in range(NT):
                b = proj_tok(wz, j)
                s.act(ztmp, b, AF.Tanh, scale=0.5)
                s.stt(ztmp, ztmp, 1.0, b, ALU.add, ALU.mult)
                s.stt(og[:, j, :], og[:, j, :], 1.0, ztmp, ALU.add, ALU.mult)
                yield

        def ml_rec(hp):
            qkT, vext, og = qkT2[hp % 2], vext2[hp % 2], og2[hp % 2]
            btr = banks[2].bitcast(BF16)
            for j in range(NT):
                ts_ = slice(j * 128, (j + 1) * 128)
                for hh in range(2):
                    h = 2 * hp + hh
                    s.tr(btr[:, hh * 128:(hh + 1) * 128], qkT[:, 2 + hh, ts_], ident_bf)
                    s.mm(banks[3][:, hh * 128:(hh + 1) * 128], qkT[:, 2 + hh, ts_], qkT[:, hh, ts_])
                yield
                for hh in range(2):
                    h = 2 * hp + hh
                    s.act(kw[hh], btr[:, hh * 128:(hh + 1) * 128], AF.Copy, scale=g_kwsc[:, j, h:h + 1])
                    s.stt(pT[hh], banks[3][:, hh * 128:(hh + 1) * 128], g_eib[:, j, h:h + 1], cm[C_TRI],
                          ALU.mult, ALU.mult)
                yield
                for hh in range(2):
                    h = 2 * hp + hh
                    bN = banks[4 + hh]
                    s.mm(bN[:, 0:257], qkT[:, hh, ts_], Cbf[:, h, 0:257], start=True, stop=False)
                    s.mm(bN[:, 0:257], pT[hh], vext[:, j, hh, 0:257], start=False, stop=True)
                    s.act(numbuf[:, j, hh, 0:257], bN[:, 0:257], AF.Copy)
                    bC = banks[6 + hh]
                    s.mm(bC[:, 0:257], kw[hh], vext[:, j, hh, 0:257])
                yield
                for hh in range(2):
                    h = 2 * hp + hh
                    bC = banks[6 + hh]
                    s.stt(Cbf[:, h, 0:257], Cst[:, h, 0:257], g_ebl[:, j, h:h + 1], bC[:, 0:257],
                          ALU.mult, ALU.add)
                    s.stt(Cst[:, h, 0:257], Cst[:, h, 0:257], g_ebl[:, j, h:h + 1], bC[:, 0:257],
                          ALU.mult, ALU.add)
                yield
            ebp = g_eb[:, :, 2 * hp:2 * hp + 2]
            s.ts("dve", ntmp, numbuf[:, :, :, 256], -1.0, None, ALU.mult)
            s.tt("dve", nden, numbuf[:, :, :, 256], ntmp, ALU.max)
            s.tt("dve", nden, nden, ebp, ALU.mult)
            s.ts("dve", nden, nden, 1.0, None, ALU.max)
            s.recip(nden, nden)
            s.tt("dve", nr, nden, ebp, ALU.mult)
            yield
            s.tt("dve", hn, numbuf[:, :, :, 0:256], nr.unsq(3).bc([128, NT, 2, 256]), ALU.mult)
            s.tt("pool", sqt, hn, hn, ALU.mult)
            yield
            s.reduce_sum(nss, sqt)
            rsqrt_(nss, nss, 1.0 / 256.0, ntmp)
            s.ts("dve", nss, nss, 0.25, None, ALU.mult)
            yield
            s.tt("dve", hn, hn, nss.unsq(3).bc([128, NT, 2, 256]), ALU.mult)
            s.tt("pool", hgb, hn.re("p a b c -> p a (b c)"), og, ALU.mult)
            yield
            if g == 0 and hp == 0:
                dump("hgb0", hgb)
                dump("num0", numbuf)
            for half in range(2):
                for cc in range(2):
                    cb = half * 2 + cc
                    for j in range(NT):
                        s.tr(btr[:, cc * 512 + j * 128:cc * 512 + (j + 1) * 128],
                             hgb[:, j, cb * 128:(cb + 1) * 128], ident_bf)
                    s.act(hgAT[:, 4 * hp + cb, :], btr[:, cc * 512:(cc + 1) * 512], AF.Copy,
                          scale=mlnwT[:, 4 * hp + cb:4 * hp + cb + 1])
                    yield

        run_all(ml_proj(0))
        for hp in range(4):
            interleave(ml_rec(hp), ml_proj(hp + 1) if hp < 3 else None, ML_RATIO)

        stop_at("mlstm")
        A.release(gmark)
        zg = A.alloc([NT, 512], F32)
        obuf = A.alloc([NT, 4, 128], F32)
        pcb = [A.alloc([TG + 4], F32) for _ in range(2)]
        cacc = A.alloc([TG], F32)
        ksil = A.alloc([TG], F32)
        sqb = A.alloc([TG], BF16)
        rk = A.alloc([TG], F32)
        qT2_ = [A.alloc([TG], BF16) for _ in range(2)]
        kT2_ = [A.alloc([TG], BF16) for _ in range(2)]
        vT2_ = [A.alloc([TG], BF16) for _ in range(2)]
        bq7 = T(banks[7].ap[:, 400:408], Buf())
        vtok = A.alloc([NT, 128], BF16)
        r0k = A.alloc([NT, 128], BF16)
        DmT = A.alloc([NT, 128], F32)
        Nm = [A.alloc([NT, 128], BF16) for _ in range(2)]
        NTm = [A.alloc([NT, 128], BF16) for _ in range(2)]
        Pm = A.alloc([NT, 128], F32)
        N0f = A.alloc([NT, 128], F32)
        DmS = A.alloc([NT, 128], F32)
        dgm = Pm
        egbc = A.alloc([NT, 128], BF16)
        Pbf = A.alloc([NT, 128], BF16)
        qdT2 = [A.alloc([NT, 128], BF16) for _ in range(2)]
        ke2 = [A.alloc([NT, 128], BF16) for _ in range(2)]
        QKd2 = [A.alloc([NT, 128], BF16) for _ in range(2)]
        wvb2 = [A.alloc([NT, 128], F32) for _ in range(2)]
        wkT2 = [A.alloc([NT, 128], BF16) for _ in range(2)]
        rqs3 = [A.alloc([NT], F32) for _ in range(3)]
        ubf = A.alloc([128], BF16)
        qss = A.alloc([NT], F32)
        qtmp = A.alloc([NT], F32)
        oss = A.alloc([NT, 4], F32)
        otmp = A.alloc([NT, 4], F32)
        hbb = A.alloc([NT, 512], BF16)
        s.memset("pool", ubf, 0.0)
        I4 = cm[C_IDENT].unsq(1).bc([128, NT, 128])
        wq3 = [None, None, None]

        def dn_conv(h):
            hl = h % 4
            hq = h // 4
            par_ = h % 2
            qT_, kT_, vT_, rqs = qT2_[par_], kT2_[par_], vT2_[par_], rqs3[h % 3]
            if hl == 0:
                wq3[0] = load_w([(0, winv[:, :, OFF_DNQ + 512 * hq:OFF_DNQ + 512 * hq + 512])])
                wq3[1] = load_w([(0, winv[:, :, OFF_DNK + 512 * hq:OFF_DNK + 512 * hq + 512])])
                wq3[2] = load_w([(0, winv[:, :, OFF_DNV + 512 * hq:OFF_DNV + 512 * hq + 512])])
            for which in range(3):
                wsl = wq3[which]
                ct = which * 16 + h
                b = proj_feat(wsl, hl * 128)
                pc = pcb[which % 2]
                s.cp("pool", pc[:, 0:4], carry[:, ct, 0:4])
                s.act(pc[:, 3:3 + TG], b, AF.Copy)
                yield
                s.ts("dve", cacc, pc[:, 0:TG], convw[:, ct, 0:1], None, ALU.mult)
                for tap in range(1, 4):
                    s.stt(cacc, pc[:, tap:tap + TG], convw[:, ct, tap:tap + 1], cacc, ALU.mult, ALU.add)
                s.cp("pool", carry[:, ct, 0:4], pc[:, TG:TG + 4])
                yield
                s.act(rk, cacc, AF.Tanh, scale=0.5)
                if which == 0:
                    s.stt(qT_, rk, 1.0, cacc, ALU.add, ALU.mult)
                    s.act(sqb, qT_, AF.Square)
                    for j in range(NT):
                        s.mm(bq7[:, j:j + 1], sqb[:, j * 128:(j + 1) * 128], ones_bf[:, 0:1])
                    s.ts("dve", qtmp, bq7[:, 0:NT], 4.0 * EPS, None, ALU.add)
                    s.tt("pool", qss, qtmp, small[:, 1:2].bc([128, NT]), ALU.pow)
                    s.ts("dve", rqs, qss, 128.0 ** -0.5, None, ALU.mult)
                elif which == 1:
                    s.stt(ksil, rk, 1.0, cacc, ALU.add, ALU.mult)
                    s.act(sqb, ksil, AF.Square)
                    for j in range(NT):
                        s.mm(bq7[:, 4 + j:5 + j], sqb[:, j * 128:(j + 1) * 128], ones_bf[:, 0:1])
                    s.ts("dve", qtmp, bq7[:, 4:4 + NT], 4.0 * EPS, None, ALU.add)
                    s.tt("pool", qss, qtmp, small[:, 1:2].bc([128, NT]), ALU.pow)
                    dgk = pcb[0][:, 0:512].re("p (a b) -> p a b", b=128)
                    s.tt("pool", dgk, I4, qss.unsq(2).bc([128, NT, 128]), ALU.mult)
                    bk_ = next_acc()
                    for j in range(NT):
                        s.mm(bk_[:, j * 128:(j + 1) * 128], cm[C_ONES], dgk[:, j, :])
                    s.tt("dve", kT_, ksil, bk_, ALU.mult)
                else:
                    s.stt(vT_, rk, 1.0, cacc, ALU.add, ALU.mult)
                yield
            if g == 0 and h == 0:
                dump("kT", kT_)
                dump("vT", vT_)
            stop_at("dn_proj")
            if hl == 3:
                wzz = load_w([(0, winv[:, :, OFF_DNZ + 512 * hq:OFF_DNZ + 512 * hq + 512])])
                for j in range(NT):
                    b = proj_tok(wzz, j)
                    s.act(zg[:, j, :], b, AF.Tanh, scale=0.5)
                    s.stt(zg[:, j, :], zg[:, j, :], 1.0, b, ALU.add, ALU.mult)
                    yield

        def dn_mat(h):
            hl = h % 4
            hq = h // 4
            par_ = h % 2
            qT_, kT_, vT_ = qT2_[par_], kT2_[par_], vT2_[par_]
            qdT, ke, QKd, wvb, wkT = qdT2[par_], ke2[par_], QKd2[par_], wvb2[par_], wkT2[par_]
            btr = banks[2].bitcast(BF16)
            for j in range(NT):
                s.tr(btr[:, j * 128:(j + 1) * 128], vT_[:, j * 128:(j + 1) * 128], ident_bf)
                s.tr(btr[:, 512 + j * 128:512 + (j + 1) * 128], kT_[:, j * 128:(j + 1) * 128], ident_bf)
            s.act(vtok, btr[:, 0:512].re("p (a b) -> p a b", b=128), AF.Copy, scale=0.5)
            ktr = btr[:, 512:1024].re("p (a b) -> p a b", b=128)
            for j in range(NT):
                s.act(r0k[:, j, :], ktr[:, j, :], AF.Copy, scale=g_egam[:, j, h:h + 1])
                s.act(ke[:, j, :], ktr[:, j, :], AF.Copy, scale=g_kesc[:, j, h:h + 1])
            stop_at("dn_tok")
            yield
            s.tt("pool", dgm, I4, gsum[:, :, 16 + h:17 + h].bc([128, NT, 128]), ALU.mult)
            bA = banks[3]
            bB = banks[4]
            for j in range(NT):
                js = slice(j * 128, (j + 1) * 128)
                s.mm(bA[:, js], cm[C_NEGONES], dgm[:, j, :])
            for j in range(NT):
                js = slice(j * 128, (j + 1) * 128)
                s.mm(bB[:, js], cm[C_NEGONES], dgm[:, j, :], start=True, stop=False)
                s.mm(bB[:, js], cm[C_IDENT], cm[C_NEGMASK], start=False, stop=True)
            yield
            s.act(egbc, bA.re("p (a b) -> p a b", b=128), AF.Exp)
            s.tt("dve", qdT, qT_.re("p (a b) -> p a b", b=128), egbc, ALU.mult)
            for j in range(NT):
                s.act(DmT[:, j, :], bB[:, j * 128:(j + 1) * 128], AF.Exp, bias=gsum[:, j, 16 + h:17 + h])
            s.tt("pool", DmS, DmT, cm[C_STRICT].unsq(1).bc([128, NT, 128]), ALU.mult)
            bK = banks[5]
            bQ = banks[3]
            for j in range(NT):
                js = slice(j * 128, (j + 1) * 128)
                s.mm(bK[:, js], kT_[:, js], kT_[:, js])
                s.mm(bQ[:, js], kT_[:, js], qT_[:, js])
            yield
            N0 = Nm[0]
            s.tt("dve", N0f, bK.re("p (a b) -> p a b", b=128), gbeta[:, :, h:h + 1].bc([128, NT, 128]), ALU.mult)
            s.tt("pool", N0f, N0f, DmS, ALU.mult)
            s.tt("dve", QKd, bQ.re("p (a b) -> p a b", b=128), DmT, ALU.mult)
            yield
            s.cp("act", N0, N0f)
            s.tt("pool", Pm, I4, N0f, ALU.subtract)
            yield
            bT = banks[4].bitcast(BF16)
            for j in range(NT):
                s.tr(bT[:, j * 128:(j + 1) * 128], N0[:, j, :], ident_bf)
            NT0 = NTm[0]
            s.cp("act", NT0, bT[:, 0:512].re("p (a b) -> p a b", b=128))
            s.cp("dve", Pbf, Pm)
            yield
            cur, curT = N0, NT0
            b1 = banks[3]
            b2 = banks[4]
            b3 = banks[5]

            def squares(lev, cur, curT):
                for j in range(NT):
                    js = slice(j * 128, (j + 1) * 128)
                    s.mm(b2[:, js], cur[:, j, :], curT[:, j, :])
                if lev < 4:
                    for j in range(NT):
                        js = slice(j * 128, (j + 1) * 128)
                        s.mm(b1[:, js], curT[:, j, :], cur[:, j, :])
            squares(0, cur, curT)
            yield
            for lev in range(5):
                nxt, nxtT = Nm[(lev + 1) % 2], NTm[(lev + 1) % 2]
                s.cp("act", nxtT, b2.re("p (a b) -> p a b", b=128))
                if lev < 4:
                    s.cp("dve", nxt, b1.re("p (a b) -> p a b", b=128))
                yield
                for j in range(NT):
                    js = slice(j * 128, (j + 1) * 128)
                    s.mm(b3[:, js], nxtT[:, j, :], Pbf[:, j, :])
                if lev < 4:
                    squares(lev + 1, nxt, nxtT)
                yield
                s.tt("dve", Pbf, b3.re("p (a b) -> p a b", b=128), Pm, ALU.add)
                if lev < 4:
                    s.tt("dve", Pm, b3.re("p (a b) -> p a b", b=128), Pm, ALU.add)
                cur, curT = nxt, nxtT
                yield
            if g == 0 and h == 0:
                dump("Pm", Pbf)
            stop_at("dn_neu")
            bW = banks[3]
            bX = banks[4]
            for j in range(NT):
                js = slice(j * 128, (j + 1) * 128)
                s.mm(bW[:, js], Pbf[:, j, :], vtok[:, j, :])
                s.mm(bX[:, js], r0k[:, j, :], Pbf[:, j, :])
            yield
            s.tt("dve", wvb, bW.re("p (a b) -> p a b", b=128), gbeta[:, :, h:h + 1].bc([128, NT, 128]), ALU.mult)
            s.cp("act", wkT, bX.re("p (a b) -> p a b", b=128))
            stop_at("dn_w")
            yield
        def dn_rec(h):
            hl = h % 4
            hq = h // 4
            par_ = h % 2
            qdT, ke, QKd, wvb, wkT, rqs = qdT2[par_], ke2[par_], QKd2[par_], wvb2[par_], wkT2[par_], rqs3[h % 3]
            for j in range(NT):
                for hf in range(2):
                    r = slice(64 * hf, 64 * hf + 64)
                    bU = banks[7][r, 0:128]
                    bO = banks[7][r, 128:256]
                    bS_ = banks[6][:, 0:128]
                    s.mm(bU, wkT[:, j, r], Sbf[:, h, :])
                    yield
                    s.stt(ubf[r, :], bU, gnegbeta[r, j, h:h + 1], wvb[r, j, :], ALU.mult, ALU.add)
                    yield
                    s.mm(bO, qdT[:, j, r], Sbf[:, h, :], start=True, stop=False)
                    s.mm(bO, QKd[:, j, r], ubf, start=False, stop=True)
                    s.mm(bS_, ke[r, j, :], ubf[r, :])
                    yield
                    s.act(obuf[r, j, hl, :], bO, AF.Copy, scale=rqs[r, j:j + 1])
                    s.stt(Sbf[:, h, :], Sst[:, h, :], g_egl[:, j, 16 * hf + h:16 * hf + h + 1], bS_,
                          ALU.mult, ALU.add)
                    s.stt(Sst[:, h, :], Sst[:, h, :], g_egl[:, j, 16 * hf + h:16 * hf + h + 1], bS_,
                          ALU.mult, ALU.add)
                    stop_at("dn_rec1")
                    yield
            if hl == 3:
                if g == 0 and hq == 0:
                    dump("obuf0", obuf)
                for j in range(NT):
                    sqj = pcb[0][:, 0:512].re("p (a b) -> p a b", b=128)
                    s.tt("pool", sqj, obuf[:, j, :, :], obuf[:, j, :, :], ALU.mult)
                    s.reduce_sum(oss[:, j, :], sqj)
                    yield
                rsqrt_(oss, oss, 1.0 / 128.0, otmp)
                s.ts("dve", oss, oss, 0.5, None, ALU.mult)
                s.tt("dve", obuf, obuf, oss.unsq(3).bc([128, NT, 4, 128]), ALU.mult)
                yield
                s.tt("pool", obuf, obuf, dnnw.unsq(1).unsq(1).bc([128, NT, 4, 128]), ALU.mult)
                s.tt("dve", hbb, obuf.re("p a b c -> p a (b c)"), zg, ALU.mult)
                yield
                btr = banks[2].bitcast(BF16)
                for half in range(2):
                    for cc in range(2):
                        cb = half * 2 + cc
                        for j in range(NT):
                            s.tr(btr[:, cc * 512 + j * 128:cc * 512 + (j + 1) * 128],
                                 hbb[:, j, cb * 128:(cb + 1) * 128], ident_bf)
                        s.cp(ev_eng(), hgBT[:, 4 * hq + cb, :], btr[:, cc * 512:(cc + 1) * 512])
                        yield

        run_all(dn_conv(0))
        interleave(dn_mat(0), dn_conv(1), DN_RATIO)
        for h in range(16):
            interleave3(dn_rec(h), dn_mat(h + 1) if h < 15 else None, dn_conv(h + 2) if h < 14 else None)
        if g == 0:
            dump("hgAT", hgAT)
            dump("hgBT", hgBT)

        stop_at("dn")
        A.release(gmark)
        yg = A.alloc([NT, 2048], F32)
        sga = A.alloc([512], F32)
        t1 = [A.alloc([512], F32) for _ in range(2)]
        fss = A.alloc([NT, 2], F32)
        ftmp = A.alloc([NT, 2], F32)
        s.memset("dve", fss, 1.0)
        fjunk = A.alloc([2048], BF16)
        vecbuf = A.alloc([2048], F32)
        s.dma("sp", vecbuf, T(par_d[:, P_FNW:P_FNW + 2048]))
        for j in range(NT):
            s.dma("sp", yg[:, j, :], xg[t0 + j * 128:t0 + (j + 1) * 128, :])
        sga4 = A.alloc([NT, 512], F32)
        for db in range(4):
            for br in range(2):
                plan.append([(0, winv[:, :, (OFF_GA if br == 0 else OFF_GB) + 512 * db:
                                        (OFF_GA if br == 0 else OFF_GB) + 512 * db + 512])])
                plan.append([(0, woutv[:, 16 * br:16 * br + 16, db * 512:(db + 1) * 512])])
        for db in range(4):
            dsl = slice(db * 512, (db + 1) * 512)
            for br in range(2):
                hsrc = hgAT if br == 0 else hgBT
                wgt = get_w()
                for j in range(NT):
                    b = proj_tok(wgt, j)
                    s.act(sga4[:, j, :], b, AF.Tanh, scale=0.5)
                wot = get_w()
                for j in range(NT):
                    b2 = proj_tok(wot, j, src=hsrc)
                    tt_ = t1[j % 2]
                    s.stt(tt_, sga4[:, j, :], 1.0, b2, ALU.add, ALU.mult)
                    s.stt(yg[:, j, dsl], tt_, 0.5, yg[:, j, dsl], ALU.mult, ALU.add)
        for j in range(NT):
            s.act(fjunk, yg[:, j, :], AF.Square, accum_out=fss[:, j, 0:1])
            rsqrt_(fss[:, j, :], fss[:, j, :], 1.0 / D_MODEL, ftmp[:, j, :])
            s.stt(yg[:, j, :], yg[:, j, :], fss[:, j, 0:1], vecbuf, ALU.mult, ALU.mult)
            s.out_dmas.append(s.dma("sp", outg[t0 + j * 128:t0 + (j + 1) * 128, :], yg[:, j, :]))
        stop_at("group0")

    try:
        for g in range(NG):
            body(g)
    except _Stop:
        pass
    s.emit()
    return nc


def make_params(inputs):
    f = np.float32
    p = np.zeros((128, NPARAM), f)
    p[:, P_NW:P_NW + 2048] = inputs["norm_w"][0][None, :]
    p[:, P_FNW:P_FNW + 2048] = inputs["final_norm_w"][None, :]
    p[:, P_MLNW:P_MLNW + 16] = inputs["ml_norm_w"][0].reshape(16, 128).T
    p[:, P_DNNW:P_DNNW + 128] = inputs["dn_norm_w"][0][None, :]
    p[:, P_GBIAS:P_GBIAS + 8] = inputs["ml_i_bias"][0][None, :]
    p[:, P_GBIAS + 8:P_GBIAS + 16] = inputs["ml_f_bias"][0][None, :]
    p[:, P_GBIAS + 32:P_GBIAS + 48] = inputs["dn_dt_bias"][0][None, :]
    p[:, P_ALOG:P_ALOG + 16] = inputs["dn_a_log"][0][None, :]
    cw = inputs["dn_conv_w"][0]
    p[:, P_CONV:P_CONV + 192] = cw.reshape(4, 48, 128).transpose(2, 1, 0).reshape(128, 192)
    return p


_NC_CACHE = {}


def kernel(**inputs):
    inputs = {k: np.asarray(v) for k, v in inputs.items()}
    x = np.ascontiguousarray(inputs["x"], dtype=np.float32)
    w_in = np.ascontiguousarray(inputs["w_in"][0], dtype=np.float32)
    w_out = np.ascontiguousarray(inputs["w_out"][0], dtype=np.float32)
    params = make_params(inputs)
    consts = host_consts()
    if "nc" not in _NC_CACHE:
        _NC_CACHE["nc"] = build()
    nc = _NC_CACHE["nc"]
    n = x.shape[0]
    in_maps = [{"x": x[b], "w_in": w_in, "w_out": w_out, "params": params, "consts": consts}
               for b in range(n)]
    res = run_bass_kernel_spmd(nc, in_maps, core_ids=list(range(n)))
    return np.stack([r["out"] for r in res.results], axis=0).astype(np.float32)
```

```python
import contextlib
import numpy as np
import concourse.bass as bass
import concourse.mybir as mybir
from concourse.bass_utils import run_bass_kernel_spmd

F32 = mybir.dt.float32
BF16 = mybir.dt.bfloat16
F32R = mybir.dt.float32r
AF = mybir.ActivationFunctionType
ALU = mybir.AluOpType
AX = mybir.AxisListType

D_MODEL = 2048
SEQ = 2048
IN_DIM = 20528
NG = 4
TG = 512
NT = 4
EPS = 1e-6
OFF_MLQ, OFF_MLK, OFF_MLV, OFF_MLO, OFF_MLZ = 0, 1024, 2048, 4096, 6144
OFF_MLI = 8192
OFF_DNQ, OFF_DNK, OFF_DNV = 8208, 8208 + 2048, 8208 + 4096
OFF_DNZ = 14352
OFF_DNB = 16400
OFF_GA, OFF_GB = 16432, 18480

ENGS = ("pe", "act", "dve", "pool", "sp")


class Buf:
    __slots__ = ("writers", "readers")

    def __init__(self):
        self.writers = {}
        self.readers = {}


class T:
    __slots__ = ("ap", "buf")

    def __init__(self, ap, buf=None):
        self.ap = ap
        self.buf = buf if buf is not None else Buf()

    def __getitem__(self, key):
        return T(self.ap[key], self.buf)

    def bc(self, shape):
        return T(self.ap.to_broadcast(list(shape)), self.buf)

    def unsq(self, axis):
        return T(self.ap.unsqueeze(axis), self.buf)

    def re(self, pattern, **kw):
        return T(self.ap.rearrange(pattern, **kw), self.buf)

    def bitcast(self, dt):
        return T(self.ap.bitcast(dt), self.buf)


def _ap(x):
    return x.ap if isinstance(x, T) else x


def _bufs(*xs):
    return [x.buf for x in xs if isinstance(x, T)]


class Op:
    __slots__ = ("eng", "idx", "fn", "waits", "signal", "is_dma", "dma_sem", "dma_val", "dma_prev")

    def __init__(self, eng, idx, fn, is_dma):
        self.eng = eng
        self.idx = idx
        self.fn = fn
        self.waits = []
        self.signal = False
        self.is_dma = is_dma
        self.dma_sem = None
        self.dma_val = None
        self.dma_prev = None


class Sched:
    def __init__(self, nc, n_dma_sems=32):
        self.nc = nc
        self.ops = {e: [] for e in ENGS}
        self.seen = {e: {} for e in ENGS}
        self.n_dma_sems = n_dma_sems
        self.dma_rr = 0
        self.dma_last = [None] * n_dma_sems
        self.dma_count = [0] * n_dma_sems
        self.out_dmas = []

    def _need(self, op, key, val):
        e = op.eng
        if isinstance(key, str):
            if key == e and e == "pe":
                return
            if self.seen[e].get(key, -1) >= val:
                return
            self.seen[e][key] = val
            self.ops[key][val].signal = True
            op.waits.append(("eng", key, val))
        else:
            if key in self.seen[e]:
                return
            self.seen[e][key] = 1
            op.waits.append(("dma", val))

    def op(self, eng, fn, reads=(), writes=(), dma=False):
        o = Op(eng, len(self.ops[eng]), fn, dma)
        for b in reads:
            for k, v in b.writers.items():
                self._need(o, k, v)
        for b in writes:
            for k, v in b.writers.items():
                self._need(o, k, v)
            for k, v in b.readers.items():
                if k == eng and not dma:
                    continue
                self._need(o, k, v)
        self.ops[eng].append(o)
        if dma:
            k = self.dma_rr
            self.dma_rr = (k + 1) % self.n_dma_sems
            o.dma_sem = k
            o.dma_prev = self.dma_last[k]
            self.dma_count[k] += 1
            o.dma_val = 16 * self.dma_count[k]
            self.dma_last[k] = o
            key = ("dma", id(o))
            for b in reads:
                b.readers[key] = o
            for b in writes:
                b.writers = {key: o}
                b.readers = {}
        else:
            for b in reads:
                b.readers[eng] = o.idx
            for b in writes:
                b.writers = {eng: o.idx}
                b.readers = {}
        return o

    def dma(self, q, out, in_):
        return self.op(q, lambda e, o=_ap(out), i=_ap(in_): e.dma_start(out=o, in_=i),
                       _bufs(in_), _bufs(out), dma=True)

    def mm(self, out, lhsT, rhs, start=True, stop=True):
        return self.op("pe", lambda e, o=_ap(out), l=_ap(lhsT), r=_ap(rhs): e.matmul(
            o, lhsT=l, rhs=r, start=start, stop=stop), _bufs(lhsT, rhs), _bufs(out))

    def tr(self, out, in_, ident):
        return self.op("pe", lambda e, o=_ap(out), i=_ap(in_), d=_ap(ident): e.transpose(o, i, d),
                       _bufs(in_, ident), _bufs(out))

    def act(self, out, in_, func, bias=0.0, scale=1.0, accum_out=None, eng="act"):
        kw = {}
        if accum_out is not None:
            kw["accum_out"] = _ap(accum_out)
        return self.op("act", lambda e, o=_ap(out), i=_ap(in_), b=_ap(bias), s=_ap(scale): e.activation(
            o, i, func, bias=b, scale=s, **kw), _bufs(in_, bias, scale), _bufs(out, accum_out))

    def tt(self, eng, out, in0, in1, op):
        return self.op(eng, lambda e, o=_ap(out), a=_ap(in0), b=_ap(in1): e.tensor_tensor(o, a, b, op),
                       _bufs(in0, in1), _bufs(out))

    def ts(self, eng, out, in0, s1, s2, op0, op1=None):
        if op1 is None and isinstance(s1, T) and eng == "dve":
            s2, op1 = 0.0, ALU.add
        if op1 is None:
            return self.op(eng, lambda e, o=_ap(out), a=_ap(in0), x=_ap(s1): e.tensor_scalar(
                o, a, x, None, op0), _bufs(in0, s1), _bufs(out))
        return self.op(eng, lambda e, o=_ap(out), a=_ap(in0), x=_ap(s1), y=_ap(s2): e.tensor_scalar(
            o, a, x, y, op0, op1), _bufs(in0, s1, s2), _bufs(out))

    def stt(self, out, in0, scalar, in1, op0, op1):
        return self.op("dve", lambda e, o=_ap(out), a=_ap(in0), x=_ap(scalar), b=_ap(in1): e.scalar_tensor_tensor(
            o, a, x, b, op0, op1), _bufs(in0, scalar, in1), _bufs(out))

    def cp(self, eng, out, in_):
        if eng == "act":
            return self.act(out, in_, AF.Copy)
        return self.op(eng, lambda e, o=_ap(out), i=_ap(in_): e.tensor_copy(o, i), _bufs(in_), _bufs(out))

    def memset(self, eng, out, val):
        return self.op(eng, lambda e, o=_ap(out): e.memset(o, val), (), _bufs(out))

    def recip(self, out, in_):
        return self.op("dve", lambda e, o=_ap(out), i=_ap(in_): e.reciprocal(o, i), _bufs(in_), _bufs(out))

    def reduce_sum(self, out, in_):
        return self.op("dve", lambda e, o=_ap(out), i=_ap(in_): e.reduce_sum(o, i, AX.X), _bufs(in_), _bufs(out))

    def emit(self):
        nc = self.nc
        with contextlib.ExitStack() as st:
            esem = {e: st.enter_context(nc.semaphore("s_" + e)) for e in ENGS}
            dsem = [st.enter_context(nc.semaphore("d%d" % i)) for i in range(self.n_dma_sems)]
            block = st.enter_context(nc.Block())
            signum = {}
            for e in ENGS:
                c = 0
                for o in self.ops[e]:
                    if o.signal and not o.is_dma:
                        c += 1
                    signum[(e, o.idx)] = c

            def run(e):
                def body(engine):
                    for o in self.ops[e]:
                        for w in o.waits:
                            if w[0] == "eng":
                                engine.wait_ge(esem[w[1]], signum[(w[1], w[2])])
                            else:
                                d = w[1]
                                engine.wait_ge(dsem[d.dma_sem], d.dma_val)
                        if o.is_dma and o.dma_prev is not None:
                            engine.wait_ge(dsem[o.dma_sem], o.dma_prev.dma_val)
                        ins = o.fn(engine)
                        if o.is_dma:
                            ins.then_inc(dsem[o.dma_sem], 16)
                        elif o.signal:
                            ins.then_inc(esem[e], 1)
                    if e == "sp":
                        for d in self.out_dmas:
                            engine.wait_ge(dsem[d.dma_sem], d.dma_val)
                return body

            block.tensor(run("pe"))
            block.scalar(run("act"))
            block.vector(run("dve"))
            block.gpsimd(run("pool"))
            block.sync(run("sp"))


class Arena:
    def __init__(self, nc, words):
        self.t = nc.alloc_sbuf_tensor("arena", [128, words], F32)
        self.ap = self.t.ap()
        self.top = 0
        self.words = words
        self.hist = []

    def alloc(self, shape, dt):
        n = int(np.prod(shape))
        bpe = 2 if dt == BF16 else 4
        w = (n * bpe + 3) // 4
        w = (w + 7) // 8 * 8
        assert self.top + w <= self.words, ("arena overflow", self.top, w, self.words)
        a = self.ap[:, self.top:self.top + w]
        nb = Buf()
        lo, hi = self.top, self.top + w
        keep = []
        for (s0, e0, b0) in self.hist:
            if s0 < hi and lo < e0:
                for src, dst in ((b0.writers, nb.writers), (b0.readers, nb.readers)):
                    for k, v in src.items():
                        if isinstance(k, str):
                            dst[k] = max(dst.get(k, -1), v)
                        else:
                            dst[k] = v
                if s0 < lo or e0 > hi:
                    keep.append((s0, e0, b0))
            else:
                keep.append((s0, e0, b0))
        keep.append((lo, hi, nb))
        self.hist = keep
        self.top += w
        if dt != F32:
            a = a.bitcast(dt)
        a = a[:, 0:n]
        if len(shape) == 2:
            a = a.rearrange("p (a b) -> p a b", b=shape[1])
        elif len(shape) == 3:
            a = a.rearrange("p (a b c) -> p a b c", b=shape[1], c=shape[2])
        return T(a, nb)

    def mark(self):
        return self.top

    def release(self, m):
        self.top = m


C_IDENT, C_TRI, C_ONES, C_NEGONES, C_TRIDN, C_BLK, C_H0, C_H1, C_NEGMASK, C_STRICT = range(10)
NCONST = 10
P_NW, P_FNW, P_MLNW, P_DNNW, P_GBIAS, P_ALOG, P_CONV = 0, 2048, 4096, 4112, 4240, 4288, 4304
NPARAM = 4304 + 192


def host_consts():
    s = np.arange(128)[:, None]
    t = np.arange(128)[None, :]
    same = (s // 64) == (t // 64)
    m = np.zeros((NCONST, 128, 128), np.float32)
    m[C_IDENT] = (s == t)
    m[C_TRI] = (s <= t)
    m[C_ONES] = 1.0
    m[C_NEGONES] = -1.0
    m[C_TRIDN] = (s <= t) & same
    m[C_BLK] = same
    m[C_H0] = (s < 64) & (t >= 0)
    m[C_H1] = (s >= 64) & (t >= 0)
    m[C_NEGMASK] = np.where((s <= t) & same, 0.0, -30000.0)
    m[C_STRICT] = (s < t) & same
    return np.ascontiguousarray(m.transpose(1, 0, 2).reshape(128, NCONST * 128))


def build(debug=None):
    nc = bass.Bass("TRN2", target_bir_lowering=False)
    x_d = nc.dram_tensor("x", [SEQ, D_MODEL], F32, kind="ExternalInput").ap()
    win_d = nc.dram_tensor("w_in", [D_MODEL, IN_DIM], F32, kind="ExternalInput").ap()
    wout_d = nc.dram_tensor("w_out", [4096, D_MODEL], F32, kind="ExternalInput").ap()
    par_d = nc.dram_tensor("params", [128, NPARAM], F32, kind="ExternalInput").ap()
    con_d = nc.dram_tensor("consts", [128, NCONST * 128], F32, kind="ExternalInput").ap()
    out_d = nc.dram_tensor("out", [SEQ, D_MODEL], F32, kind="ExternalOutput").ap()
    dbg_d = {}
    if debug:
        for name, shape in debug.items():
            if name.startswith("_"):
                continue
            dbg_d[name] = nc.dram_tensor("dbg_" + name, [128] + list(shape), F32, kind="ExternalOutput").ap()

    winv = win_d.rearrange("(kc p) n -> p kc n", p=128)
    woutv = wout_d.rearrange("(jc p) n -> p jc n", p=128)
    xg = T(x_d)
    outg = T(out_d)

    s = Sched(nc)
    A = Arena(nc, 53100)
    banks = [T(nc.alloc_psum_tensor("bank%d" % i, [128, 512], F32).ap()) for i in range(8)]

    consts = A.alloc([NCONST, 128], F32)
    cm = [consts[:, i, :] for i in range(NCONST)]
    ident_bf = A.alloc([128], BF16)
    ones_bf = A.alloc([128], BF16)
    par = A.alloc([NPARAM - 4096], F32)
    PO = 4096

    def pslice(off, n):
        return par[:, off - PO:off - PO + n]
    mlnwT = pslice(P_MLNW, 16)
    dnnw = pslice(P_DNNW, 128)
    gbias = pslice(P_GBIAS, 48)
    alog = pslice(P_ALOG, 16)
    convw = T(par.ap[:, P_CONV - PO:P_CONV - PO + 192].rearrange("p (a b) -> p a b", b=4), par.buf)
    expA = A.alloc([16], F32)
    xT = A.alloc([16, TG], BF16)
    wg = A.alloc([16, 48], BF16)
    NRING = 3
    ring = [A.alloc([16, 512], BF16) for _ in range(NRING)]
    hgAT = A.alloc([16, TG], BF16)
    hgBT = A.alloc([16, TG], BF16)
    Cst = A.alloc([8, 258], F32)
    Cbf = A.alloc([8, 258], BF16)
    Sst = A.alloc([16, 128], F32)
    Sbf = A.alloc([16, 128], BF16)
    carry = A.alloc([48, 4], F32)
    graw = A.alloc([NT, 48], F32)
    gth = A.alloc([NT, 16], F32)
    glf = A.alloc([NT, 8], F32)
    gbeta = A.alloc([NT, 16], F32)
    gnegbeta = A.alloc([NT, 16], F32)
    gneg = A.alloc([NT, 16], F32)
    gsum = A.alloc([NT, 80], F32)
    gtmp = A.alloc([NT, 16], F32)
    g_eib = A.alloc([NT, 8], F32)
    g_eb = A.alloc([NT, 8], F32)
    g_kwsc = A.alloc([NT, 8], F32)
    g_ebl = A.alloc([NT, 8], F32)
    g_egam = A.alloc([NT, 16], F32)
    g_kesc = A.alloc([NT, 16], F32)
    g_egl = A.alloc([NT, 32], F32)
    small = A.alloc([64], F32)
    base_mark = A.mark()

    s.dma("sp", consts, T(con_d))
    s.dma("sp", par, T(par_d[:, PO:NPARAM]))
    s.cp("dve", ident_bf, cm[C_IDENT])
    s.cp("dve", ones_bf, cm[C_ONES])
    s.act(expA, alog, AF.Exp)
    s.memset("pool", Cst, 0.0)
    s.memset("pool", Cbf, 0.0)
    s.memset("pool", Sst, 0.0)
    s.memset("pool", Sbf, 0.0)
    s.memset("pool", carry, 0.0)
    s.memset("pool", small, -0.5)
    s.dma("pool", wg[:, :, 0:16], T(winv[:, :, OFF_MLI:OFF_MLI + 16]))
    s.dma("pool", wg[:, :, 16:48], T(winv[:, :, OFF_DNB:OFF_DNB + 32]))

    ring_i = [0]

    def load_w(src_list):
        slot = ring[ring_i[0] % NRING]
        ring_i[0] += 1
        for off, src in src_list:
            n = src.shape[2]
            k = src.shape[1]
            s.dma("pool", slot[:, 0:k, off:off + n], T(src))
        return slot

    plan = []
    plan_loaded = []

    def get_w():
        while len(plan_loaded) < NRING and plan:
            plan_loaded.append(load_w(plan.pop(0)))
        return plan_loaded.pop(0)

    acc_i = [0]

    def next_acc():
        b = banks[acc_i[0] % 2]
        acc_i[0] += 1
        return b

    ev_i = [0]

    def ev_eng():
        ev_i[0] += 1
        return "act" if ev_i[0] % 2 else "dve"

    def proj_tok(w, j, ncols=512, src=None, nk=16):
        src = xT if src is None else src
        b = next_acc()
        for kc in range(nk):
            s.mm(b[:, 0:ncols], src[:, kc, j * 128:(j + 1) * 128], w[:, kc, 0:ncols],
                 start=(kc == 0), stop=(kc == nk - 1))
        return b

    def proj_feat(w, c0):
        b = next_acc()
        for kc in range(16):
            s.mm(b, w[:, kc, c0:c0 + 128], xT[:, kc, :], start=(kc == 0), stop=(kc == 15))
        return b

    def rsqrt_(out, in_, scale, tmp, eps=EPS):
        s.ts("dve", tmp, in_, scale, eps, ALU.mult, ALU.add)
        mh = small[:, 1:2]
        shp = list(tmp.ap.shape)
        for ax in range(2, len(shp)):
            mh = mh.unsq(ax)
        s.tt("pool", out, tmp, mh.bc(shp), ALU.pow)

    def dump(name, t):
        if debug and name in debug:
            s.out_dmas.append(s.dma("pool", T(dbg_d[name]), t))

    class _Stop(Exception):
        pass

    ML_RATIO = (3, 1)
    DN_RATIO = (1, 1)

    def run_all(gen):
        for _ in gen:
            pass

    def interleave3(*gens):
        live = [g_ for g_ in gens if g_ is not None]
        while live:
            for g_ in list(live):
                try:
                    next(g_)
                except StopIteration:
                    live.remove(g_)

    def interleave(ga, gb, ratio):
        na, nb_ = ratio
        da = ga is None
        db = gb is None
        while not (da and db):
            for _ in range(na):
                if not da:
                    try:
                        next(ga)
                    except StopIteration:
                        da = True
            for _ in range(nb_):
                if not db:
                    try:
                        next(gb)
                    except StopIteration:
                        db = True

    def stop_at(name):
        if debug and debug.get("_stop") == name:
            raise _Stop()

    def body(g):
        t0 = g * TG
        A.release(base_mark)
        vecbuf = A.alloc([2048], F32)
        s.dma("sp", vecbuf, T(par_d[:, P_NW:P_NW + 2048]))
        xs = [A.alloc([2048], F32) for _ in range(2)]
        junk = A.alloc([2048], BF16)
        ssx = A.alloc([NT, 2], F32)
        rsx = A.alloc([NT, 2], F32)
        tmpx = A.alloc([NT, 2], F32)
        s.memset("dve", ssx, 1.0)
        for j in range(NT):
            xt = xs[j % 2]
            s.dma("sp", xt, xg[t0 + j * 128:t0 + (j + 1) * 128, :])
            s.act(junk, xt, AF.Square, accum_out=ssx[:, j, 0:1])
            rsqrt_(rsx[:, j, :], ssx[:, j, :], 1.0 / D_MODEL, tmpx[:, j, :])
            s.stt(xt, xt, rsx[:, j, 0:1], vecbuf, ALU.mult, ALU.mult)
            for q in range(4):
                bk = banks[4 + q]
                for c in range(4):
                    kc = q * 4 + c
                    s.tr(bk[:, c * 128:(c + 1) * 128], xt[:, kc * 128:(kc + 1) * 128], cm[C_IDENT])
                s.cp(ev_eng(), xT[:, q * 4:(q + 1) * 4, j * 128:(j + 1) * 128],
                     bk.re("p (a b) -> p a b", b=128))
        if g == 0:
            dump("xT", xT)
        stop_at("phaseA")

        for j in range(NT):
            b = proj_tok(wg, j, 48)
            s.cp("dve", graw[:, j, :], b[:, 0:48])
        s.tt("dve", graw, graw, gbias.unsq(1).bc([128, NT, 48]), ALU.add)
        s.act(gth, graw[:, :, 0:16], AF.Tanh, scale=1.0 / 15.0)
        s.act(glf, gth[:, :, 8:16], AF.Exp, scale=-15.0)
        s.act(gneg, graw[:, :, 32:48], AF.Exp)
        s.act(glf, glf, AF.Ln, bias=1.0)
        s.act(gneg, gneg, AF.Ln, bias=1.0)
        s.tt("dve", gneg, gneg, expA.unsq(1).bc([128, NT, 16]), ALU.mult)
        s.act(gbeta, graw[:, :, 16:32], AF.Tanh, scale=0.5)
        s.ts("dve", gbeta, gbeta, 0.5, 0.5, ALU.mult, ALU.add)
        s.ts("dve", gnegbeta, gbeta, -1.0, None, ALU.mult)
        bsum = banks[7]
        for j in range(NT):
            o = j * 80
            s.mm(bsum[:, o:o + 8], cm[C_TRI], glf[:, j, :])
            s.mm(bsum[:, o + 8:o + 16], cm[C_ONES], glf[:, j, :])
            s.mm(bsum[:, o + 16:o + 32], cm[C_TRIDN], gneg[:, j, :])
            s.mm(bsum[:, o + 32:o + 48], cm[C_BLK], gneg[:, j, :])
            s.mm(bsum[:, o + 48:o + 64], cm[C_H0], gneg[:, j, :])
            s.mm(bsum[:, o + 64:o + 80], cm[C_H1], gneg[:, j, :])
        s.cp("dve", gsum, bsum[:, 0:NT * 80].re("p (a b) -> p a b", b=80))
        nb = gsum[:, :, 0:8]
        nbt = gsum[:, :, 8:16]
        ngam = gsum[:, :, 16:32]
        ngblk = gsum[:, :, 32:48]
        s.stt(gtmp[:, :, 0:8], gth[:, :, 0:8], 15.0, nb, ALU.mult, ALU.add)
        s.act(g_eib, gtmp[:, :, 0:8], AF.Exp)
        s.tt("dve", gtmp[:, :, 0:8], gtmp[:, :, 0:8], nbt, ALU.subtract)
        s.act(g_kwsc, gtmp[:, :, 0:8], AF.Exp)
        s.act(g_eb, nb, AF.Exp, scale=-1.0)
        s.act(g_ebl, nbt, AF.Exp, scale=-1.0)
        s.act(g_egam, ngam, AF.Exp, scale=-1.0)
        s.tt("dve", gtmp, ngam, ngblk, ALU.subtract)
        s.act(g_kesc, gtmp, AF.Exp)
        s.act(g_egl, gsum[:, :, 48:80], AF.Exp, scale=-1.0)
        if g == 0:
            dump("gsum", gsum)
            dump("eib", g_eib)
            dump("beta", gbeta)
        stop_at("gates")

        gmark = base_mark
        A.release(gmark)
        qkT2 = [A.alloc([4, TG], BF16) for _ in range(2)]
        vext2 = [A.alloc([NT, 2, 258], BF16) for _ in range(2)]
        og2 = [A.alloc([NT, 512], F32) for _ in range(2)]
        ztmp = A.alloc([512], F32)
        numbuf = A.alloc([NT, 2, 258], F32)
        hn = A.alloc([NT, 2, 256], F32)
        sqt = hn2 = A.alloc([NT, 2, 256], F32)
        hgb = A.alloc([NT, 512], BF16)
        kw = [A.alloc([128], BF16) for _ in range(2)]
        pT = [A.alloc([128], BF16) for _ in range(2)]
        nden = A.alloc([NT, 2], F32)
        nr = A.alloc([NT, 2], F32)
        nss = A.alloc([NT, 2], F32)
        ntmp = A.alloc([NT, 2], F32)
        s.memset("pool", vext2[0], 1.0)
        s.memset("pool", vext2[1], 1.0)

        for hp_ in range(4):
            plan.append([(0, winv[:, :, OFF_MLQ + 256 * hp_:OFF_MLQ + 256 * hp_ + 256]),
                         (256, winv[:, :, OFF_MLK + 256 * hp_:OFF_MLK + 256 * hp_ + 256])])
            plan.append([(0, winv[:, :, OFF_MLV + 512 * hp_:OFF_MLV + 512 * hp_ + 512])])
            plan.append([(0, winv[:, :, OFF_MLO + 512 * hp_:OFF_MLO + 512 * hp_ + 512])])
            plan.append([(0, winv[:, :, OFF_MLZ + 512 * hp_:OFF_MLZ + 512 * hp_ + 512])])

        def ml_proj(hp):
            qkT, vext, og = qkT2[hp % 2], vext2[hp % 2], og2[hp % 2]
            wqk = get_w()
            for c4 in range(4):
                b = proj_feat(wqk, c4 * 128)
                if c4 < 2:
                    s.act(qkT[:, c4, :], b, AF.Copy, scale=128.0 ** -0.5)
                else:
                    s.cp("dve", qkT[:, c4, :], b)
                yield
            wv = get_w()
            for j in range(NT):
                b = proj_tok(wv, j)
                s.cp(ev_eng(), vext[:, j, :, 0:256], b.re("p (a b) -> p a b", b=256))
                yield
            wo = get_w()
            for j in range(NT):
                b = proj_tok(wo, j)
                s.act(og[:, j, :], b, AF.Tanh, scale=0.5)
                yield
            wz = get_w()
            for j in range(NT):
                b = proj_tok(wz, j)
                s.act(ztmp, b, AF.Tanh, scale=0.5)
                s.stt(ztmp, ztmp, 1.0, b, ALU.add, ALU.mult)
                s.stt(og[:, j, :], og[:, j, :], 1.0, ztmp, ALU.add, ALU.mult)
                yield

        def ml_rec(hp):
            qkT, vext, og = qkT2[hp % 2], vext2[hp % 2], og2[hp % 2]
            btr = banks[2].bitcast(BF16)
            for j in range(NT):
                ts_ = slice(j * 128, (j + 1) * 128)
                for hh in range(2):
                    h = 2 * hp + hh
                    s.tr(btr[:, hh * 128:(hh + 1) * 128], qkT[:, 2 + hh, ts_], ident_bf)
                    s.mm(banks[3][:, hh * 128:(hh + 1) * 128], qkT[:, 2 + hh, ts_], qkT[:, hh, ts_])
                yield
                for hh in range(2):
                    h = 2 * hp + hh
                    s.act(kw[hh], btr[:, hh * 128:(hh + 1) * 128], AF.Copy, scale=g_kwsc[:, j, h:h + 1])
                    s.stt(pT[hh], banks[3][:, hh * 128:(hh + 1) * 128], g_eib[:, j, h:h + 1], cm[C_TRI],
                          ALU.mult, ALU.mult)
                yield
                for hh in range(2):
                    h = 2 * hp + hh
                    bN = banks[4 + hh]
                    s.mm(bN[:, 0:257], qkT[:, hh, ts_], Cbf[:, h, 0:257], start=True, stop=False)
                    s.mm(bN[:, 0:257], pT[hh], vext[:, j, hh, 0:257], start=False, stop=True)
                    s.act(numbuf[:, j, hh, 0:257], bN[:, 0:257], AF.Copy)
                    bC = banks[6 + hh]
                    s.mm(bC[:, 0:257], kw[hh], vext[:, j, hh, 0:257])
                yield
                for hh in range(2):
                    h = 2 * hp + hh
                    bC = banks[6 + hh]
                    s.stt(Cbf[:, h, 0:257], Cst[:, h, 0:257], g_ebl[:, j, h:h + 1], bC[:, 0:257],
                          ALU.mult, ALU.add)
                    s.stt(Cst[:, h, 0:257], Cst[:, h, 0:257], g_ebl[:, j, h:h + 1], bC[:, 0:257],
                          ALU.mult, ALU.add)
                yield
            ebp = g_eb[:, :, 2 * hp:2 * hp + 2]
            s.ts("dve", ntmp, numbuf[:, :, :, 256], -1.0, None, ALU.mult)
            s.tt("dve", nden, numbuf[:, :, :, 256], ntmp, ALU.max)
            s.tt("dve", nden, nden, ebp, ALU.mult)
            s.ts("dve", nden, nden, 1.0, None, ALU.max)
            s.recip(nden, nden)
            s.tt("dve", nr, nden, ebp, ALU.mult)
            yield
            s.tt("dve", hn, numbuf[:, :, :, 0:256], nr.unsq(3).bc([128, NT, 2, 256]), ALU.mult)
            s.tt("pool", sqt, hn, hn, ALU.mult)
            yield
            s.reduce_sum(nss, sqt)
            rsqrt_(nss, nss, 1.0 / 256.0, ntmp)
            s.ts("dve", nss, nss, 0.25, None, ALU.mult)
            yield
            s.tt("dve", hn, hn, nss.unsq(3).bc([128, NT, 2, 256]), ALU.mult)
            s.tt("pool", hgb, hn.re("p a b c -> p a (b c)"), og, ALU.mult)
            yield
            if g == 0 and hp == 0:
                dump("hgb0", hgb)
                dump("num0", numbuf)
            for half in range(2):
                for cc in range(2):
                    cb = half * 2 + cc
                    for j in range(NT):
                        s.tr(btr[:, cc * 512 + j * 128:cc * 512 + (j + 1) * 128],
                             hgb[:, j, cb * 128:(cb + 1) * 128], ident_bf)
                    s.act(hgAT[:, 4 * hp + cb, :], btr[:, cc * 512:(cc + 1) * 512], AF.Copy,
                          scale=mlnwT[:, 4 * hp + cb:4 * hp + cb + 1])
                    yield

        run_all(ml_proj(0))
        for hp in range(4):
            interleave(ml_rec(hp), ml_proj(hp + 1) if hp < 3 else None, ML_RATIO)

        stop_at("mlstm")
        A.release(gmark)
        zg = A.alloc([NT, 512], F32)
        obuf = A.alloc([NT, 4, 128], F32)
        pcb = [A.alloc([TG + 4], F32) for _ in range(2)]
        cacc = A.alloc([TG], F32)
        ksil = A.alloc([TG], F32)
        sqb = A.alloc([TG], BF16)
        rk = A.alloc([TG], F32)
        qT2_ = [A.alloc([TG], BF16) for _ in range(2)]
        kT2_ = [A.alloc([TG], BF16) for _ in range(2)]
        vT2_ = [A.alloc([TG], BF16) for _ in range(2)]
        bq7 = T(banks[7].ap[:, 400:408], Buf())
        vtok = A.alloc([NT, 128], BF16)
        r0k = A.alloc([NT, 128], BF16)
        DmT = A.alloc([NT, 128], F32)
        Nm = [A.alloc([NT, 128], BF16) for _ in range(2)]
        NTm = [A.alloc([NT, 128], BF16) for _ in range(2)]
        Pm = A.alloc([NT, 128], F32)
        N0f = A.alloc([NT, 128], F32)
        DmS = A.alloc([NT, 128], F32)
        dgm = Pm
        egbc = A.alloc([NT, 128], BF16)
        Pbf = A.alloc([NT, 128], BF16)
        qdT2 = [A.alloc([NT, 128], BF16) for _ in range(2)]
        ke2 = [A.alloc([NT, 128], BF16) for _ in range(2)]
        QKd2 = [A.alloc([NT, 128], BF16) for _ in range(2)]
        wvb2 = [A.alloc([NT, 128], F32) for _ in range(2)]
        wkT2 = [A.alloc([NT, 128], BF16) for _ in range(2)]
        rqs3 = [A.alloc([NT], F32) for _ in range(3)]
        ubf = A.alloc([128], BF16)
        qss = A.alloc([NT], F32)
        qtmp = A.alloc([NT], F32)
        oss = A.alloc([NT, 4], F32)
        otmp = A.alloc([NT, 4], F32)
        hbb = A.alloc([NT, 512], BF16)
        s.memset("pool", ubf, 0.0)
        I4 = cm[C_IDENT].unsq(1).bc([128, NT, 128])
        wq3 = [None, None, None]

        def dn_conv(h):
            hl = h % 4
            hq = h // 4
            par_ = h % 2
            qT_, kT_, vT_, rqs = qT2_[par_], kT2_[par_], vT2_[par_], rqs3[h % 3]
            if hl == 0:
                wq3[0] = load_w([(0, winv[:, :, OFF_DNQ + 512 * hq:OFF_DNQ + 512 * hq + 512])])
                wq3[1] = load_w([(0, winv[:, :, OFF_DNK + 512 * hq:OFF_DNK + 512 * hq + 512])])
                wq3[2] = load_w([(0, winv[:, :, OFF_DNV + 512 * hq:OFF_DNV + 512 * hq + 512])])
            for which in range(3):
                wsl = wq3[which]
                ct = which * 16 + h
                b = proj_feat(wsl, hl * 128)
                pc = pcb[which % 2]
                s.cp("pool", pc[:, 0:4], carry[:, ct, 0:4])
                s.act(pc[:, 3:3 + TG], b, AF.Copy)
                yield
                s.ts("dve", cacc, pc[:, 0:TG], convw[:, ct, 0:1], None, ALU.mult)
                for tap in range(1, 4):
                    s.stt(cacc, pc[:, tap:tap + TG], convw[:, ct, tap:tap + 1], cacc, ALU.mult, ALU.add)
                s.cp("pool", carry[:, ct, 0:4], pc[:, TG:TG + 4])
                yield
                s.act(rk, cacc, AF.Tanh, scale=0.5)
                if which == 0:
                    s.stt(qT_, rk, 1.0, cacc, ALU.add, ALU.mult)
                    s.act(sqb, qT_, AF.Square)
                    for j in range(NT):
                        s.mm(bq7[:, j:j + 1], sqb[:, j * 128:(j + 1) * 128], ones_bf[:, 0:1])
                    s.ts("dve", qtmp, bq7[:, 0:NT], 4.0 * EPS, None, ALU.add)
                    s.tt("pool", qss, qtmp, small[:, 1:2].bc([128, NT]), ALU.pow)
                    s.ts("dve", rqs, qss, 128.0 ** -0.5, None, ALU.mult)
                elif which == 1:
                    s.stt(ksil, rk, 1.0, cacc, ALU.add, ALU.mult)
                    s.act(sqb, ksil, AF.Square)
                    for j in range(NT):
                        s.mm(bq7[:, 4 + j:5 + j], sqb[:, j * 128:(j + 1) * 128], ones_bf[:, 0:1])
                    s.ts("dve", qtmp, bq7[:, 4:4 + NT], 4.0 * EPS, None, ALU.add)
                    s.tt("pool", qss, qtmp, small[:, 1:2].bc([128, NT]), ALU.pow)
                    dgk = pcb[0][:, 0:512].re("p (a b) -> p a b", b=128)
                    s.tt("pool", dgk, I4, qss.unsq(2).bc([128, NT, 128]), ALU.mult)
                    bk_ = next_acc()
                    for j in range(NT):
                        s.mm(bk_[:, j * 128:(j + 1) * 128], cm[C_ONES], dgk[:, j, :])
                    s.tt("dve", kT_, ksil, bk_, ALU.mult)
                else:
                    s.stt(vT_, rk, 1.0, cacc, ALU.add, ALU.mult)
                yield
            if g == 0 and h == 0:
                dump("kT", kT_)
                dump("vT", vT_)
            stop_at("dn_proj")
            if hl == 3:
                wzz = load_w([(0, winv[:, :, OFF_DNZ + 512 * hq:OFF_DNZ + 512 * hq + 512])])
                for j in range(NT):
                    b = proj_tok(wzz, j)
                    s.act(zg[:, j, :], b, AF.Tanh, scale=0.5)
                    s.stt(zg[:, j, :], zg[:, j, :], 1.0, b, ALU.add, ALU.mult)
                    yield

        def dn_mat(h):
            hl = h % 4
            hq = h // 4
            par_ = h % 2
            qT_, kT_, vT_ = qT2_[par_], kT2_[par_], vT2_[par_]
            qdT, ke, QKd, wvb, wkT = qdT2[par_], ke2[par_], QKd2[par_], wvb2[par_], wkT2[par_]
            btr = banks[2].bitcast(BF16)
            for j in range(NT):
                s.tr(btr[:, j * 128:(j + 1) * 128], vT_[:, j * 128:(j + 1) * 128], ident_bf)
                s.tr(btr[:, 512 + j * 128:512 + (j + 1) * 128], kT_[:, j * 128:(j + 1) * 128], ident_bf)
            s.act(vtok, btr[:, 0:512].re("p (a b) -> p a b", b=128), AF.Copy, scale=0.5)
            ktr = btr[:, 512:1024].re("p (a b) -> p a b", b=128)
            for j in range(NT):
                s.act(r0k[:, j, :], ktr[:, j, :], AF.Copy, scale=g_egam[:, j, h:h + 1])
                s.act(ke[:, j, :], ktr[:, j, :], AF.Copy, scale=g_kesc[:, j, h:h + 1])
            stop_at("dn_tok")
            yield
            s.tt("pool", dgm, I4, gsum[:, :, 16 + h:17 + h].bc([128, NT, 128]), ALU.mult)
            bA = banks[3]
            bB = banks[4]
            for j in range(NT):
                js = slice(j * 128, (j + 1) * 128)
                s.mm(bA[:, js], cm[C_NEGONES], dgm[:, j, :])
            for j in range(NT):
                js = slice(j * 128, (j + 1) * 128)
                s.mm(bB[:, js], cm[C_NEGONES], dgm[:, j, :], start=True, stop=False)
                s.mm(bB[:, js], cm[C_IDENT], cm[C_NEGMASK], start=False, stop=True)
            yield
            s.act(egbc, bA.re("p (a b) -> p a b", b=128), AF.Exp)
            s.tt("dve", qdT, qT_.re("p (a b) -> p a b", b=128), egbc, ALU.mult)
            for j in range(NT):
                s.act(DmT[:, j, :], bB[:, j * 128:(j + 1) * 128], AF.Exp, bias=gsum[:, j, 16 + h:17 + h])
            s.tt("pool", DmS, DmT, cm[C_STRICT].unsq(1).bc([128, NT, 128]), ALU.mult)
            bK = banks[5]
            bQ = banks[3]
            for j in range(NT):
                js = slice(j * 128, (j + 1) * 128)
                s.mm(bK[:, js], kT_[:, js], kT_[:, js])
                s.mm(bQ[:, js], kT_[:, js], qT_[:, js])
            yield
            N0 = Nm[0]
            s.tt("dve", N0f, bK.re("p (a b) -> p a b", b=128), gbeta[:, :, h:h + 1].bc([128, NT, 128]), ALU.mult)
            s.tt("pool", N0f, N0f, DmS, ALU.mult)
            s.tt("dve", QKd, bQ.re("p (a b) -> p a b", b=128), DmT, ALU.mult)
            yield
            s.cp("act", N0, N0f)
            s.tt("pool", Pm, I4, N0f, ALU.subtract)
            yield
            bT = banks[4].bitcast(BF16)
            for j in range(NT):
                s.tr(bT[:, j * 128:(j + 1) * 128], N0[:, j, :], ident_bf)
            NT0 = NTm[0]
            s.cp("act", NT0, bT[:, 0:512].re("p (a b) -> p a b", b=128))
            s.cp("dve", Pbf, Pm)
            yield
            cur, curT = N0, NT0
            b1 = banks[3]
            b2 = banks[4]
            b3 = banks[5]

            def squares(lev, cur, curT):
                for j in range(NT):
                    js = slice(j * 128, (j + 1) * 128)
                    s.mm(b2[:, js], cur[:, j, :], curT[:, j, :])
                if lev < 4:
                    for j in range(NT):
                        js = slice(j * 128, (j + 1) * 128)
                        s.mm(b1[:, js], curT[:, j, :], cur[:, j, :])
            squares(0, cur, curT)
            yield
            for lev in range(5):
                nxt, nxtT = Nm[(lev + 1) % 2], NTm[(lev + 1) % 2]
                s.cp("act", nxtT, b2.re("p (a b) -> p a b", b=128))
                if lev < 4:
                    s.cp("act", nxt, b1.re("p (a b) -> p a b", b=128))
                yield
                for j in range(NT):
                    js = slice(j * 128, (j + 1) * 128)
                    s.mm(b3[:, js], nxtT[:, j, :], Pbf[:, j, :])
                if lev < 4:
                    squares(lev + 1, nxt, nxtT)
                yield
                s.tt("dve", Pbf, b3.re("p (a b) -> p a b", b=128), Pm, ALU.add)
                if lev < 4:
                    s.tt("dve", Pm, b3.re("p (a b) -> p a b", b=128), Pm, ALU.add)
                cur, curT = nxt, nxtT
                yield
            if g == 0 and h == 0:
                dump("Pm", Pbf)
            stop_at("dn_neu")
            bW = banks[3]
            bX = banks[4]
            for j in range(NT):
                js = slice(j * 128, (j + 1) * 128)
                s.mm(bW[:, js], Pbf[:, j, :], vtok[:, j, :])
                s.mm(bX[:, js], r0k[:, j, :], Pbf[:, j, :])
            yield
            s.tt("dve", wvb, bW.re("p (a b) -> p a b", b=128), gbeta[:, :, h:h + 1].bc([128, NT, 128]), ALU.mult)
            s.cp("act", wkT, bX.re("p (a b) -> p a b", b=128))
            stop_at("dn_w")
            yield
        def dn_rec(h):
            hl = h % 4
            hq = h // 4
            par_ = h % 2
            qdT, ke, QKd, wvb, wkT, rqs = qdT2[par_], ke2[par_], QKd2[par_], wvb2[par_], wkT2[par_], rqs3[h % 3]
            for j in range(NT):
                for hf in range(2):
                    r = slice(64 * hf, 64 * hf + 64)
                    bU = banks[7][r, 0:128]
                    bO = banks[7][r, 128:256]
                    bS_ = banks[6][:, 0:128]
                    s.mm(bU, wkT[:, j, r], Sbf[:, h, :])
                    yield
                    s.stt(ubf[r, :], bU, gnegbeta[r, j, h:h + 1], wvb[r, j, :], ALU.mult, ALU.add)
                    yield
                    s.mm(bO, qdT[:, j, r], Sbf[:, h, :], start=True, stop=False)
                    s.mm(bO, QKd[:, j, r], ubf, start=False, stop=True)
                    s.mm(bS_, ke[r, j, :], ubf[r, :])
                    yield
                    s.act(obuf[r, j, hl, :], bO, AF.Copy, scale=rqs[r, j:j + 1])
                    s.stt(Sbf[:, h, :], Sst[:, h, :], g_egl[:, j, 16 * hf + h:16 * hf + h + 1], bS_,
                          ALU.mult, ALU.add)
                    s.stt(Sst[:, h, :], Sst[:, h, :], g_egl[:, j, 16 * hf + h:16 * hf + h + 1], bS_,
                          ALU.mult, ALU.add)
                    stop_at("dn_rec1")
                    yield
            if hl == 3:
                if g == 0 and hq == 0:
                    dump("obuf0", obuf)
                for j in range(NT):
                    sqj = pcb[0][:, 0:512].re("p (a b) -> p a b", b=128)
                    s.tt("pool", sqj, obuf[:, j, :, :], obuf[:, j, :, :], ALU.mult)
                    s.reduce_sum(oss[:, j, :], sqj)
                    yield
                rsqrt_(oss, oss, 1.0 / 128.0, otmp)
                s.ts("dve", oss, oss, 0.5, None, ALU.mult)
                s.tt("dve", obuf, obuf, oss.unsq(3).bc([128, NT, 4, 128]), ALU.mult)
                yield
                s.tt("pool", obuf, obuf, dnnw.unsq(1).unsq(1).bc([128, NT, 4, 128]), ALU.mult)
                s.tt("dve", hbb, obuf.re("p a b c -> p a (b c)"), zg, ALU.mult)
                yield
                btr = banks[2].bitcast(BF16)
                for half in range(2):
                    for cc in range(2):
                        cb = half * 2 + cc
                        for j in range(NT):
                            s.tr(btr[:, cc * 512 + j * 128:cc * 512 + (j + 1) * 128],
                                 hbb[:, j, cb * 128:(cb + 1) * 128], ident_bf)
                        s.cp(ev_eng(), hgBT[:, 4 * hq + cb, :], btr[:, cc * 512:(cc + 1) * 512])
                        yield

        run_all(dn_conv(0))
        interleave(dn_mat(0), dn_conv(1), DN_RATIO)
        for h in range(16):
            interleave3(dn_rec(h), dn_mat(h + 1) if h < 15 else None, dn_conv(h + 2) if h < 14 else None)
        if g == 0:
            dump("hgAT", hgAT)
            dump("hgBT", hgBT)

        stop_at("dn")
        A.release(gmark)
        yg = A.alloc([NT, 2048], F32)
        sga = A.alloc([512], F32)
        t1 = [A.alloc([512], F32) for _ in range(2)]
        fss = A.alloc([NT, 2], F32)
        ftmp = A.alloc([NT, 2], F32)
        s.memset("dve", fss, 1.0)
        fjunk = A.alloc([2048], BF16)
        vecbuf = A.alloc([2048], F32)
        s.dma("sp", vecbuf, T(par_d[:, P_FNW:P_FNW + 2048]))
        for j in range(NT):
            s.dma("sp", yg[:, j, :], xg[t0 + j * 128:t0 + (j + 1) * 128, :])
        sga4 = A.alloc([NT, 512], F32)
        for db in range(4):
            for br in range(2):
                plan.append([(0, winv[:, :, (OFF_GA if br == 0 else OFF_GB) + 512 * db:
                                        (OFF_GA if br == 0 else OFF_GB) + 512 * db + 512])])
                plan.append([(0, woutv[:, 16 * br:16 * br + 16, db * 512:(db + 1) * 512])])
        for db in range(4):
            dsl = slice(db * 512, (db + 1) * 512)
            for br in range(2):
                hsrc = hgAT if br == 0 else hgBT
                wgt = get_w()
                for j in range(NT):
                    b = proj_tok(wgt, j)
                    s.act(sga4[:, j, :], b, AF.Tanh, scale=0.5)
                wot = get_w()
                for j in range(NT):
                    b2 = proj_tok(wot, j, src=hsrc)
                    tt_ = t1[j % 2]
                    s.stt(tt_, sga4[:, j, :], 1.0, b2, ALU.add, ALU.mult)
                    s.stt(yg[:, j, dsl], tt_, 0.5, yg[:, j, dsl], ALU.mult, ALU.add)
        for j in range(NT):
            s.act(fjunk, yg[:, j, :], AF.Square, accum_out=fss[:, j, 0:1])
            rsqrt_(fss[:, j, :], fss[:, j, :], 1.0 / D_MODEL, ftmp[:, j, :])
            s.stt(yg[:, j, :], yg[:, j, :], fss[:, j, 0:1], vecbuf, ALU.mult, ALU.mult)
            s.out_dmas.append(s.dma("sp", outg[t0 + j * 128:t0 + (j + 1) * 128, :], yg[:, j, :]))
        stop_at("group0")

    try:
        for g in range(NG):
            body(g)
    except _Stop:
        pass
    s.emit()
    return nc


def make_params(inputs):
    f = np.float32
    p = np.zeros((128, NPARAM), f)
    p[:, P_NW:P_NW + 2048] = inputs["norm_w"][0][None, :]
    p[:, P_FNW:P_FNW + 2048] = inputs["final_norm_w"][None, :]
    p[:, P_MLNW:P_MLNW + 16] = inputs["ml_norm_w"][0].reshape(16, 128).T
    p[:, P_DNNW:P_DNNW + 128] = inputs["dn_norm_w"][0][None, :]
    p[:, P_GBIAS:P_GBIAS + 8] = inputs["ml_i_bias"][0][None, :]
    p[:, P_GBIAS + 8:P_GBIAS + 16] = inputs["ml_f_bias"][0][None, :]
    p[:, P_GBIAS + 32:P_GBIAS + 48] = inputs["dn_dt_bias"][0][None, :]
    p[:, P_ALOG:P_ALOG + 16] = inputs["dn_a_log"][0][None, :]
    cw = inputs["dn_conv_w"][0]
    p[:, P_CONV:P_CONV + 192] = cw.reshape(4, 48, 128).transpose(2, 1, 0).reshape(128, 192)
    return p


_NC_CACHE = {}


def kernel(**inputs):
    inputs = {k: np.asarray(v) for k, v in inputs.items()}
    x = np.ascontiguousarray(inputs["x"], dtype=np.float32)
    w_in = np.ascontiguousarray(inputs["w_in"][0], dtype=np.float32)
    w_out = np.ascontiguousarray(inputs["w_out"][0], dtype=np.float32)
    params = make_params(inputs)
    consts = host_consts()
    if "nc" not in _NC_CACHE:
        _NC_CACHE["nc"] = build()
    nc = _NC_CACHE["nc"]
    n = x.shape[0]
    in_maps = [{"x": x[b], "w_in": w_in, "w_out": w_out, "params": params, "consts": consts}
               for b in range(n)]
    res = run_bass_kernel_spmd(nc, in_maps, core_ids=list(range(n)))
    return np.stack([r["out"] for r in res.results], axis=0).astype(np.float32)
```

```python
import contextlib
import numpy as np
import concourse.bass as bass
import concourse.mybir as mybir
from concourse.bass_utils import run_bass_kernel_spmd

F32 = mybir.dt.float32
BF16 = mybir.dt.bfloat16
F32R = mybir.dt.float32r
AF = mybir.ActivationFunctionType
ALU = mybir.AluOpType
AX = mybir.AxisListType

D_MODEL = 2048
SEQ = 2048
IN_DIM = 20528
NG = 4
TG = 512
NT = 4
EPS = 1e-6
OFF_MLQ, OFF_MLK, OFF_MLV, OFF_MLO, OFF_MLZ = 0, 1024, 2048, 4096, 6144
OFF_MLI = 8192
OFF_DNQ, OFF_DNK, OFF_DNV = 8208, 8208 + 2048, 8208 + 4096
OFF_DNZ = 14352
OFF_DNB = 16400
OFF_GA, OFF_GB = 16432, 18480

ENGS = ("pe", "act", "dve", "pool", "sp")


class Buf:
    __slots__ = ("writers", "readers")

    def __init__(self):
        self.writers = {}
        self.readers = {}


class T:
    __slots__ = ("ap", "buf")

    def __init__(self, ap, buf=None):
        self.ap = ap
        self.buf = buf if buf is not None else Buf()

    def __getitem__(self, key):
        return T(self.ap[key], self.buf)

    def bc(self, shape):
        return T(self.ap.to_broadcast(list(shape)), self.buf)

    def unsq(self, axis):
        return T(self.ap.unsqueeze(axis), self.buf)

    def re(self, pattern, **kw):
        return T(self.ap.rearrange(pattern, **kw), self.buf)

    def bitcast(self, dt):
        return T(self.ap.bitcast(dt), self.buf)


def _ap(x):
    return x.ap if isinstance(x, T) else x


def _bufs(*xs):
    return [x.buf for x in xs if isinstance(x, T)]


class Op:
    __slots__ = ("eng", "idx", "fn", "waits", "signal", "is_dma", "dma_sem", "dma_val", "dma_prev")

    def __init__(self, eng, idx, fn, is_dma):
        self.eng = eng
        self.idx = idx
        self.fn = fn
        self.waits = []
        self.signal = False
        self.is_dma = is_dma
        self.dma_sem = None
        self.dma_val = None
        self.dma_prev = None


class Sched:
    def __init__(self, nc, n_dma_sems=32):
        self.nc = nc
        self.ops = {e: [] for e in ENGS}
        self.seen = {e: {} for e in ENGS}
        self.n_dma_sems = n_dma_sems
        self.dma_rr = 0
        self.dma_last = [None] * n_dma_sems
        self.dma_count = [0] * n_dma_sems
        self.out_dmas = []

    def _need(self, op, key, val):
        e = op.eng
        if isinstance(key, str):
            if key == e and e == "pe":
                return
            if self.seen[e].get(key, -1) >= val:
                return
            self.seen[e][key] = val
            self.ops[key][val].signal = True
            op.waits.append(("eng", key, val))
        else:
            if key in self.seen[e]:
                return
            self.seen[e][key] = 1
            op.waits.append(("dma", val))

    def op(self, eng, fn, reads=(), writes=(), dma=False):
        o = Op(eng, len(self.ops[eng]), fn, dma)
        for b in reads:
            for k, v in b.writers.items():
                self._need(o, k, v)
        for b in writes:
            for k, v in b.writers.items():
                self._need(o, k, v)
            for k, v in b.readers.items():
                if k == eng and not dma:
                    continue
                self._need(o, k, v)
        self.ops[eng].append(o)
        if dma:
            k = self.dma_rr
            self.dma_rr = (k + 1) % self.n_dma_sems
            o.dma_sem = k
            o.dma_prev = self.dma_last[k]
            self.dma_count[k] += 1
            o.dma_val = 16 * self.dma_count[k]
            self.dma_last[k] = o
            key = ("dma", id(o))
            for b in reads:
                b.readers[key] = o
            for b in writes:
                b.writers = {key: o}
                b.readers = {}
        else:
            for b in reads:
                b.readers[eng] = o.idx
            for b in writes:
                b.writers = {eng: o.idx}
                b.readers = {}
        return o

    def dma(self, q, out, in_):
        return self.op(q, lambda e, o=_ap(out), i=_ap(in_): e.dma_start(out=o, in_=i),
                       _bufs(in_), _bufs(out), dma=True)

    def mm(self, out, lhsT, rhs, start=True, stop=True):
        return self.op("pe", lambda e, o=_ap(out), l=_ap(lhsT), r=_ap(rhs): e.matmul(
            o, lhsT=l, rhs=r, start=start, stop=stop), _bufs(lhsT, rhs), _bufs(out))

    def tr(self, out, in_, ident):
        return self.op("pe", lambda e, o=_ap(out), i=_ap(in_), d=_ap(ident): e.transpose(o, i, d),
                       _bufs(in_, ident), _bufs(out))

    def act(self, out, in_, func, bias=0.0, scale=1.0, accum_out=None, eng="act"):
        kw = {}
        if accum_out is not None:
            kw["accum_out"] = _ap(accum_out)
        return self.op("act", lambda e, o=_ap(out), i=_ap(in_), b=_ap(bias), s=_ap(scale): e.activation(
            o, i, func, bias=b, scale=s, **kw), _bufs(in_, bias, scale), _bufs(out, accum_out))

    def tt(self, eng, out, in0, in1, op):
        return self.op(eng, lambda e, o=_ap(out), a=_ap(in0), b=_ap(in1): e.tensor_tensor(o, a, b, op),
                       _bufs(in0, in1), _bufs(out))

    def ts(self, eng, out, in0, s1, s2, op0, op1=None):
        if op1 is None and isinstance(s1, T) and eng == "dve":
            s2, op1 = 0.0, ALU.add
        if op1 is None:
            return self.op(eng, lambda e, o=_ap(out), a=_ap(in0), x=_ap(s1): e.tensor_scalar(
                o, a, x, None, op0), _bufs(in0, s1), _bufs(out))
        return self.op(eng, lambda e, o=_ap(out), a=_ap(in0), x=_ap(s1), y=_ap(s2): e.tensor_scalar(
            o, a, x, y, op0, op1), _bufs(in0, s1, s2), _bufs(out))

    def stt(self, out, in0, scalar, in1, op0, op1):
        return self.op("dve", lambda e, o=_ap(out), a=_ap(in0), x=_ap(scalar), b=_ap(in1): e.scalar_tensor_tensor(
            o, a, x, b, op0, op1), _bufs(in0, scalar, in1), _bufs(out))

    def cp(self, eng, out, in_):
        if eng == "act":
            return self.act(out, in_, AF.Copy)
        return self.op(eng, lambda e, o=_ap(out), i=_ap(in_): e.tensor_copy(o, i), _bufs(in_), _bufs(out))

    def memset(self, eng, out, val):
        return self.op(eng, lambda e, o=_ap(out): e.memset(o, val), (), _bufs(out))

    def recip(self, out, in_):
        return self.op("dve", lambda e, o=_ap(out), i=_ap(in_): e.reciprocal(o, i), _bufs(in_), _bufs(out))

    def reduce_sum(self, out, in_):
        return self.op("dve", lambda e, o=_ap(out), i=_ap(in_): e.reduce_sum(o, i, AX.X), _bufs(in_), _bufs(out))

    def emit(self):
        nc = self.nc
        with contextlib.ExitStack() as st:
            esem = {e: st.enter_context(nc.semaphore("s_" + e)) for e in ENGS}
            dsem = [st.enter_context(nc.semaphore("d%d" % i)) for i in range(self.n_dma_sems)]
            block = st.enter_context(nc.Block())
            signum = {}
            for e in ENGS:
                c = 0
                for o in self.ops[e]:
                    if o.signal and not o.is_dma:
                        c += 1
                    signum[(e, o.idx)] = c

            def run(e):
                def body(engine):
                    for o in self.ops[e]:
                        for w in o.waits:
                            if w[0] == "eng":
                                engine.wait_ge(esem[w[1]], signum[(w[1], w[2])])
                            else:
                                d = w[1]
                                engine.wait_ge(dsem[d.dma_sem], d.dma_val)
                        if o.is_dma and o.dma_prev is not None:
                            engine.wait_ge(dsem[o.dma_sem], o.dma_prev.dma_val)
                        ins = o.fn(engine)
                        if o.is_dma:
                            ins.then_inc(dsem[o.dma_sem], 16)
                        elif o.signal:
                            ins.then_inc(esem[e], 1)
                    if e == "sp":
                        for d in self.out_dmas:
                            engine.wait_ge(dsem[d.dma_sem], d.dma_val)
                return body

            block.tensor(run("pe"))
            block.scalar(run("act"))
            block.vector(run("dve"))
            block.gpsimd(run("pool"))
            block.sync(run("sp"))


class Arena:
    def __init__(self, nc, words):
        self.t = nc.alloc_sbuf_tensor("arena", [128, words], F32)
        self.ap = self.t.ap()
        self.top = 0
        self.words = words
        self.hist = []

    def alloc(self, shape, dt):
        n = int(np.prod(shape))
        bpe = 2 if dt == BF16 else 4
        w = (n * bpe + 3) // 4
        w = (w + 7) // 8 * 8
        assert self.top + w <= self.words, ("arena overflow", self.top, w, self.words)
        a = self.ap[:, self.top:self.top + w]
        nb = Buf()
        lo, hi = self.top, self.top + w
        keep = []
        for (s0, e0, b0) in self.hist:
            if s0 < hi and lo < e0:
                for src, dst in ((b0.writers, nb.writers), (b0.readers, nb.readers)):
                    for k, v in src.items():
                        if isinstance(k, str):
                            dst[k] = max(dst.get(k, -1), v)
                        else:
                            dst[k] = v
                if s0 < lo or e0 > hi:
                    keep.append((s0, e0, b0))
            else:
                keep.append((s0, e0, b0))
        keep.append((lo, hi, nb))
        self.hist = keep
        self.top += w
        if dt != F32:
            a = a.bitcast(dt)
        a = a[:, 0:n]
        if len(shape) == 2:
            a = a.rearrange("p (a b) -> p a b", b=shape[1])
        elif len(shape) == 3:
            a = a.rearrange("p (a b c) -> p a b c", b=shape[1], c=shape[2])
        return T(a, nb)

    def mark(self):
        return self.top

    def release(self, m):
        self.top = m


C_IDENT, C_TRI, C_ONES, C_NEGONES, C_TRIDN, C_BLK, C_H0, C_H1, C_NEGMASK, C_STRICT = range(10)
NCONST = 10
P_NW, P_FNW, P_MLNW, P_DNNW, P_GBIAS, P_ALOG, P_CONV = 0, 2048, 4096, 4112, 4240, 4288, 4304
NPARAM = 4304 + 192


def host_consts():
    s = np.arange(128)[:, None]
    t = np.arange(128)[None, :]
    same = (s // 64) == (t // 64)
    m = np.zeros((NCONST, 128, 128), np.float32)
    m[C_IDENT] = (s == t)
    m[C_TRI] = (s <= t)
    m[C_ONES] = 1.0
    m[C_NEGONES] = -1.0
    m[C_TRIDN] = (s <= t) & same
    m[C_BLK] = same
    m[C_H0] = (s < 64) & (t >= 0)
    m[C_H1] = (s >= 64) & (t >= 0)
    m[C_NEGMASK] = np.where((s <= t) & same, 0.0, -30000.0)
    m[C_STRICT] = (s < t) & same
    return np.ascontiguousarray(m.transpose(1, 0, 2).reshape(128, NCONST * 128))


def build(debug=None):
    nc = bass.Bass("TRN2", target_bir_lowering=False)
    x_d = nc.dram_tensor("x", [SEQ, D_MODEL], F32, kind="ExternalInput").ap()
    win_d = nc.dram_tensor("w_in", [D_MODEL, IN_DIM], F32, kind="ExternalInput").ap()
    wout_d = nc.dram_tensor("w_out", [4096, D_MODEL], F32, kind="ExternalInput").ap()
    par_d = nc.dram_tensor("params", [128, NPARAM], F32, kind="ExternalInput").ap()
    con_d = nc.dram_tensor("consts", [128, NCONST * 128], F32, kind="ExternalInput").ap()
    out_d = nc.dram_tensor("out", [SEQ, D_MODEL], F32, kind="ExternalOutput").ap()
    dbg_d = {}
    if debug:
        for name, shape in debug.items():
            if name.startswith("_"):
                continue
            dbg_d[name] = nc.dram_tensor("dbg_" + name, [128] + list(shape), F32, kind="ExternalOutput").ap()

    winv = win_d.rearrange("(kc p) n -> p kc n", p=128)
    woutv = wout_d.rearrange("(jc p) n -> p jc n", p=128)
    xg = T(x_d)
    outg = T(out_d)

    s = Sched(nc)
    A = Arena(nc, 53100)
    banks = [T(nc.alloc_psum_tensor("bank%d" % i, [128, 512], F32).ap()) for i in range(8)]

    consts = A.alloc([NCONST, 128], F32)
    cm = [consts[:, i, :] for i in range(NCONST)]
    ident_bf = A.alloc([128], BF16)
    ones_bf = A.alloc([128], BF16)
    par = A.alloc([NPARAM - 4096], F32)
    PO = 4096

    def pslice(off, n):
        return par[:, off - PO:off - PO + n]
    mlnwT = pslice(P_MLNW, 16)
    dnnw = pslice(P_DNNW, 128)
    gbias = pslice(P_GBIAS, 48)
    alog = pslice(P_ALOG, 16)
    convw = T(par.ap[:, P_CONV - PO:P_CONV - PO + 192].rearrange("p (a b) -> p a b", b=4), par.buf)
    expA = A.alloc([16], F32)
    xT = A.alloc([16, TG], BF16)
    wg = A.alloc([16, 48], BF16)
    NRING = 3
    ring = [A.alloc([16, 512], BF16) for _ in range(NRING)]
    hgAT = A.alloc([16, TG], BF16)
    hgBT = A.alloc([16, TG], BF16)
    Cst = A.alloc([8, 258], F32)
    Cbf = A.alloc([8, 258], BF16)
    Sst = A.alloc([16, 128], F32)
    Sbf = A.alloc([16, 128], BF16)
    carry = A.alloc([48, 4], F32)
    graw = A.alloc([NT, 48], F32)
    gth = A.alloc([NT, 16], F32)
    glf = A.alloc([NT, 8], F32)
    gbeta = A.alloc([NT, 16], F32)
    gnegbeta = A.alloc([NT, 16], F32)
    gneg = A.alloc([NT, 16], F32)
    gsum = A.alloc([NT, 80], F32)
    gtmp = A.alloc([NT, 16], F32)
    g_eib = A.alloc([NT, 8], F32)
    g_eb = A.alloc([NT, 8], F32)
    g_kwsc = A.alloc([NT, 8], F32)
    g_ebl = A.alloc([NT, 8], F32)
    g_egam = A.alloc([NT, 16], F32)
    g_kesc = A.alloc([NT, 16], F32)
    g_egl = A.alloc([NT, 32], F32)
    small = A.alloc([64], F32)
    base_mark = A.mark()

    s.dma("sp", consts, T(con_d))
    s.dma("sp", par, T(par_d[:, PO:NPARAM]))
    s.cp("dve", ident_bf, cm[C_IDENT])
    s.cp("dve", ones_bf, cm[C_ONES])
    s.act(expA, alog, AF.Exp)
    s.memset("pool", Cst, 0.0)
    s.memset("pool", Cbf, 0.0)
    s.memset("pool", Sst, 0.0)
    s.memset("pool", Sbf, 0.0)
    s.memset("pool", carry, 0.0)
    s.memset("pool", small, -0.5)
    s.dma("pool", wg[:, :, 0:16], T(winv[:, :, OFF_MLI:OFF_MLI + 16]))
    s.dma("pool", wg[:, :, 16:48], T(winv[:, :, OFF_DNB:OFF_DNB + 32]))

    ring_i = [0]

    def load_w(src_list):
        slot = ring[ring_i[0] % NRING]
        ring_i[0] += 1
        for off, src in src_list:
            n = src.shape[2]
            k = src.shape[1]
            s.dma("pool", slot[:, 0:k, off:off + n], T(src))
        return slot

    plan = []
    plan_loaded = []

    def get_w():
        while len(plan_loaded) < NRING and plan:
            plan_loaded.append(load_w(plan.pop(0)))
        return plan_loaded.pop(0)

    acc_i = [0]

    def next_acc():
        b = banks[acc_i[0] % 2]
        acc_i[0] += 1
        return b

    ev_i = [0]

    def ev_eng():
        ev_i[0] += 1
        return "act" if ev_i[0] % 2 else "dve"

    def proj_tok(w, j, ncols=512, src=None, nk=16):
        src = xT if src is None else src
        b = next_acc()
        for kc in range(nk):
            s.mm(b[:, 0:ncols], src[:, kc, j * 128:(j + 1) * 128], w[:, kc, 0:ncols],
                 start=(kc == 0), stop=(kc == nk - 1))
        return b

    def proj_feat(w, c0):
        b = next_acc()
        for kc in range(16):
            s.mm(b, w[:, kc, c0:c0 + 128], xT[:, kc, :], start=(kc == 0), stop=(kc == 15))
        return b

    def rsqrt_(out, in_, scale, tmp, eps=EPS):
        s.ts("dve", tmp, in_, scale, eps, ALU.mult, ALU.add)
        mh = small[:, 1:2]
        shp = list(tmp.ap.shape)
        for ax in range(2, len(shp)):
            mh = mh.unsq(ax)
        s.tt("pool", out, tmp, mh.bc(shp), ALU.pow)

    def dump(name, t):
        if debug and name in debug:
            s.out_dmas.append(s.dma("pool", T(dbg_d[name]), t))

    class _Stop(Exception):
        pass

    ML_RATIO = (3, 1)
    DN_RATIO = (1, 1)

    def run_all(gen):
        for _ in gen:
            pass

    def interleave3(*gens):
        live = [g_ for g_ in gens if g_ is not None]
        while live:
            for g_ in list(live):
                try:
                    next(g_)
                except StopIteration:
                    live.remove(g_)

    def interleave(ga, gb, ratio):
        na, nb_ = ratio
        da = ga is None
        db = gb is None
        while not (da and db):
            for _ in range(na):
                if not da:
                    try:
                        next(ga)
                    except StopIteration:
                        da = True
            for _ in range(nb_):
                if not db:
                    try:
                        next(gb)
                    except StopIteration:
                        db = True

    def stop_at(name):
        if debug and debug.get("_stop") == name:
            raise _Stop()

    def body(g):
        t0 = g * TG
        A.release(base_mark)
        for hp_ in range(4):
            plan.append([(0, winv[:, :, OFF_MLQ + 256 * hp_:OFF_MLQ + 256 * hp_ + 256]),
                         (256, winv[:, :, OFF_MLK + 256 * hp_:OFF_MLK + 256 * hp_ + 256])])
            plan.append([(0, winv[:, :, OFF_MLV + 512 * hp_:OFF_MLV + 512 * hp_ + 512])])
            plan.append([(0, winv[:, :, OFF_MLO + 512 * hp_:OFF_MLO + 512 * hp_ + 512])])
            plan.append([(0, winv[:, :, OFF_MLZ + 512 * hp_:OFF_MLZ + 512 * hp_ + 512])])
        while len(plan_loaded) < NRING and plan:
            plan_loaded.append(load_w(plan.pop(0)))
        vecbuf = A.alloc([2048], F32)
        s.dma("sp", vecbuf, T(par_d[:, P_NW:P_NW + 2048]))
        xs = [A.alloc([2048], F32) for _ in range(2)]
        junk = A.alloc([2048], BF16)
        ssx = A.alloc([NT, 2], F32)
        rsx = A.alloc([NT, 2], F32)
        tmpx = A.alloc([NT, 2], F32)
        s.memset("dve", ssx, 1.0)
        for j in range(NT):
            xt = xs[j % 2]
            s.dma("sp", xt, xg[t0 + j * 128:t0 + (j + 1) * 128, :])
            s.act(junk, xt, AF.Square, accum_out=ssx[:, j, 0:1])
            rsqrt_(rsx[:, j, :], ssx[:, j, :], 1.0 / D_MODEL, tmpx[:, j, :])
            s.stt(xt, xt, rsx[:, j, 0:1], vecbuf, ALU.mult, ALU.mult)
            for q in range(4):
                bk = banks[4 + q]
                for c in range(4):
                    kc = q * 4 + c
                    s.tr(bk[:, c * 128:(c + 1) * 128], xt[:, kc * 128:(kc + 1) * 128], cm[C_IDENT])
                s.cp(ev_eng(), xT[:, q * 4:(q + 1) * 4, j * 128:(j + 1) * 128],
                     bk.re("p (a b) -> p a b", b=128))
        if g == 0:
            dump("xT", xT)
        stop_at("phaseA")

        def gates_gen():
            for j in range(NT):
                b = proj_tok(wg, j, 48)
                s.cp("dve", graw[:, j, :], b[:, 0:48])
                yield
            s.tt("dve", graw, graw, gbias.unsq(1).bc([128, NT, 48]), ALU.add)
            yield
            s.act(gth, graw[:, :, 0:16], AF.Tanh, scale=1.0 / 15.0)
            yield
            s.act(glf, gth[:, :, 8:16], AF.Exp, scale=-15.0)
            yield
            s.act(gneg, graw[:, :, 32:48], AF.Exp)
            yield
            s.act(glf, glf, AF.Ln, bias=1.0)
            yield
            s.act(gneg, gneg, AF.Ln, bias=1.0)
            yield
            s.tt("dve", gneg, gneg, expA.unsq(1).bc([128, NT, 16]), ALU.mult)
            yield
            s.act(gbeta, graw[:, :, 16:32], AF.Tanh, scale=0.5)
            yield
            s.ts("dve", gbeta, gbeta, 0.5, 0.5, ALU.mult, ALU.add)
            yield
            s.ts("dve", gnegbeta, gbeta, -1.0, None, ALU.mult)
            yield
            bsum = banks[7]
            for j in range(NT):
                o = j * 80
                s.mm(bsum[:, o:o + 8], cm[C_TRI], glf[:, j, :])
                s.mm(bsum[:, o + 8:o + 16], cm[C_ONES], glf[:, j, :])
                s.mm(bsum[:, o + 16:o + 32], cm[C_TRIDN], gneg[:, j, :])
                s.mm(bsum[:, o + 32:o + 48], cm[C_BLK], gneg[:, j, :])
                s.mm(bsum[:, o + 48:o + 64], cm[C_H0], gneg[:, j, :])
                s.mm(bsum[:, o + 64:o + 80], cm[C_H1], gneg[:, j, :])
                yield
            s.cp("dve", gsum, bsum[:, 0:NT * 80].re("p (a b) -> p a b", b=80))
            yield
            nb = gsum[:, :, 0:8]
            nbt = gsum[:, :, 8:16]
            ngam = gsum[:, :, 16:32]
            ngblk = gsum[:, :, 32:48]
            s.stt(gtmp[:, :, 0:8], gth[:, :, 0:8], 15.0, nb, ALU.mult, ALU.add)
            yield
            s.act(g_eib, gtmp[:, :, 0:8], AF.Exp)
            yield
            s.tt("dve", gtmp[:, :, 0:8], gtmp[:, :, 0:8], nbt, ALU.subtract)
            yield
            s.act(g_kwsc, gtmp[:, :, 0:8], AF.Exp)
            yield
            s.act(g_eb, nb, AF.Exp, scale=-1.0)
            yield
            s.act(g_ebl, nbt, AF.Exp, scale=-1.0)
            yield
            s.act(g_egam, ngam, AF.Exp, scale=-1.0)
            yield
            s.tt("dve", gtmp, ngam, ngblk, ALU.subtract)
            yield
            s.act(g_kesc, gtmp, AF.Exp)
            yield
            s.act(g_egl, gsum[:, :, 48:80], AF.Exp, scale=-1.0)
            yield
        gg = gates_gen()

        gmark = base_mark
        A.release(gmark)
        qkT2 = [A.alloc([4, TG], BF16) for _ in range(2)]
        vext2 = [A.alloc([NT, 2, 258], BF16) for _ in range(2)]
        og2 = [A.alloc([NT, 512], F32) for _ in range(2)]
        ztmp = A.alloc([512], F32)
        numbuf = A.alloc([NT, 2, 258], F32)
        hn = A.alloc([NT, 2, 256], F32)
        sqt = hn2 = A.alloc([NT, 2, 256], F32)
        hgb = A.alloc([NT, 512], BF16)
        kw = [A.alloc([128], BF16) for _ in range(2)]
        pT = [A.alloc([128], BF16) for _ in range(2)]
        nden = A.alloc([NT, 2], F32)
        nr = A.alloc([NT, 2], F32)
        nss = A.alloc([NT, 2], F32)
        ntmp = A.alloc([NT, 2], F32)
        s.memset("pool", vext2[0], 1.0)
        s.memset("pool", vext2[1], 1.0)


        def ml_proj(hp):
            qkT, vext, og = qkT2[hp % 2], vext2[hp % 2], og2[hp % 2]
            wqk = get_w()
            for c4 in range(4):
                b = proj_feat(wqk, c4 * 128)
                if c4 < 2:
                    s.act(qkT[:, c4, :], b, AF.Copy, scale=128.0 ** -0.5)
                else:
                    s.cp("dve", qkT[:, c4, :], b)
                yield
            wv = get_w()
            for j in range(NT):
                b = proj_tok(wv, j)
                s.cp(ev_eng(), vext[:, j, :, 0:256], b.re("p (a b) -> p a b", b=256))
                yield
            wo = get_w()
            for j in range(NT):
                b = proj_tok(wo, j)
                s.act(og[:, j, :], b, AF.Tanh, scale=0.5)
                yield
            wz = get_w()
            for j in range(NT):
                b = proj_tok(wz, j)
                s.act(ztmp, b, AF.Tanh, scale=0.5)
                s.stt(ztmp, ztmp, 1.0, b, ALU.add, ALU.mult)
                s.stt(og[:, j, :], og[:, j, :], 1.0, ztmp, ALU.add, ALU.mult)
                yield

        def ml_rec(hp):
            qkT, vext, og = qkT2[hp % 2], vext2[hp % 2], og2[hp % 2]
            btr = banks[2].bitcast(BF16)
            for j in range(NT):
                ts_ = slice(j * 128, (j + 1) * 128)
                for hh in range(2):
                    h = 2 * hp + hh
                    s.tr(btr[:, hh * 128:(hh + 1) * 128], qkT[:, 2 + hh, ts_], ident_bf)
                    s.mm(banks[3][:, hh * 128:(hh + 1) * 128], qkT[:, 2 + hh, ts_], qkT[:, hh, ts_])
                yield
                for hh in range(2):
                    h = 2 * hp + hh
                    s.act(kw[hh], btr[:, hh * 128:(hh + 1) * 128], AF.Copy, scale=g_kwsc[:, j, h:h + 1])
                    s.stt(pT[hh], banks[3][:, hh * 128:(hh + 1) * 128], g_eib[:, j, h:h + 1], cm[C_TRI],
                          ALU.mult, ALU.mult)
                yield
                for hh in range(2):
                    h = 2 * hp + hh
                    bN = banks[4 + hh]
                    s.mm(bN[:, 0:257], qkT[:, hh, ts_], Cbf[:, h, 0:257], start=True, stop=False)
                    s.mm(bN[:, 0:257], pT[hh], vext[:, j, hh, 0:257], start=False, stop=True)
                    s.act(numbuf[:, j, hh, 0:257], bN[:, 0:257], AF.Copy)
                    bC = banks[6 + hh]
                    s.mm(bC[:, 0:257], kw[hh], vext[:, j, hh, 0:257])
                yield
                for hh in range(2):
                    h = 2 * hp + hh
                    bC = banks[6 + hh]
                    s.stt(Cbf[:, h, 0:257], Cst[:, h, 0:257], g_ebl[:, j, h:h + 1], bC[:, 0:257],
                          ALU.mult, ALU.add)
                    s.stt(Cst[:, h, 0:257], Cst[:, h, 0:257], g_ebl[:, j, h:h + 1], bC[:, 0:257],
                          ALU.mult, ALU.add)
                yield
            ebp = g_eb[:, :, 2 * hp:2 * hp + 2]
            s.ts("dve", ntmp, numbuf[:, :, :, 256], -1.0, None, ALU.mult)
            s.tt("dve", nden, numbuf[:, :, :, 256], ntmp, ALU.max)
            s.tt("dve", nden, nden, ebp, ALU.mult)
            s.ts("dve", nden, nden, 1.0, None, ALU.max)
            s.recip(nden, nden)
            s.tt("dve", nr, nden, ebp, ALU.mult)
            yield
            s.tt("dve", hn, numbuf[:, :, :, 0:256], nr.unsq(3).bc([128, NT, 2, 256]), ALU.mult)
            s.tt("pool", sqt, hn, hn, ALU.mult)
            yield
            s.reduce_sum(nss, sqt)
            rsqrt_(nss, nss, 1.0 / 256.0, ntmp)
            s.ts("dve", nss, nss, 0.25, None, ALU.mult)
            yield
            s.tt("dve", hn, hn, nss.unsq(3).bc([128, NT, 2, 256]), ALU.mult)
            s.tt("pool", hgb, hn.re("p a b c -> p a (b c)"), og, ALU.mult)
            yield
            if g == 0 and hp == 0:
                dump("hgb0", hgb)
                dump("num0", numbuf)
            for half in range(2):
                for cc in range(2):
                    cb = half * 2 + cc
                    for j in range(NT):
                        s.tr(btr[:, cc * 512 + j * 128:cc * 512 + (j + 1) * 128],
                             hgb[:, j, cb * 128:(cb + 1) * 128], ident_bf)
                    s.act(hgAT[:, 4 * hp + cb, :], btr[:, cc * 512:(cc + 1) * 512], AF.Copy,
                          scale=mlnwT[:, 4 * hp + cb:4 * hp + cb + 1])
                    yield

        interleave(gg, ml_proj(0), (2, 1))
        if g == 0:
            dump("gsum", gsum)
            dump("eib", g_eib)
            dump("beta", gbeta)
        stop_at("gates")
        for hp in range(4):
            interleave(ml_rec(hp), ml_proj(hp + 1) if hp < 3 else None, ML_RATIO)

        stop_at("mlstm")
        A.release(gmark)
        zg = A.alloc([NT, 512], F32)
        obuf = A.alloc([NT, 4, 128], F32)
        pcb = [A.alloc([TG + 4], F32) for _ in range(2)]
        cacc = A.alloc([TG], F32)
        ksil = A.alloc([TG], F32)
        sqb = A.alloc([TG], BF16)
        rk = A.alloc([TG], F32)
        qT2_ = [A.alloc([TG], BF16) for _ in range(2)]
        kT2_ = [A.alloc([TG], BF16) for _ in range(2)]
        vT2_ = [A.alloc([TG], BF16) for _ in range(2)]
        bq7 = T(banks[7].ap[:, 400:408], Buf())
        vtok = A.alloc([NT, 128], BF16)
        r0k = A.alloc([NT, 128], BF16)
        DmT = A.alloc([NT, 128], F32)
        Nm = [A.alloc([NT, 128], BF16) for _ in range(2)]
        NTm = [A.alloc([NT, 128], BF16) for _ in range(2)]
        Pm = A.alloc([NT, 128], F32)
        N0f = A.alloc([NT, 128], F32)
        DmS = A.alloc([NT, 128], F32)
        dgm = Pm
        egbc = A.alloc([NT, 128], BF16)
        Pbf = A.alloc([NT, 128], BF16)
        qdT2 = [A.alloc([NT, 128], BF16) for _ in range(2)]
        ke2 = [A.alloc([NT, 128], BF16) for _ in range(2)]
        QKd2 = [A.alloc([NT, 128], BF16) for _ in range(2)]
        wvb2 = [A.alloc([NT, 128], F32) for _ in range(2)]
        wkT2 = [A.alloc([NT, 128], BF16) for _ in range(2)]
        rqs3 = [A.alloc([NT], F32) for _ in range(3)]
        ubf = A.alloc([128], BF16)
        qss = A.alloc([NT], F32)
        qtmp = A.alloc([NT], F32)
        oss = A.alloc([NT, 4], F32)
        otmp = A.alloc([NT, 4], F32)
        hbb = A.alloc([NT, 512], BF16)
        s.memset("pool", ubf, 0.0)
        I4 = cm[C_IDENT].unsq(1).bc([128, NT, 128])
        wq3 = [None, None, None]

        def dn_conv(h):
            hl = h % 4
            hq = h // 4
            par_ = h % 2
            qT_, kT_, vT_, rqs = qT2_[par_], kT2_[par_], vT2_[par_], rqs3[h % 3]
            if hl == 0:
                wq3[0] = load_w([(0, winv[:, :, OFF_DNQ + 512 * hq:OFF_DNQ + 512 * hq + 512])])
                wq3[1] = load_w([(0, winv[:, :, OFF_DNK + 512 * hq:OFF_DNK + 512 * hq + 512])])
                wq3[2] = load_w([(0, winv[:, :, OFF_DNV + 512 * hq:OFF_DNV + 512 * hq + 512])])
            for which in range(3):
                wsl = wq3[which]
                ct = which * 16 + h
                b = proj_feat(wsl, hl * 128)
                pc = pcb[which % 2]
                s.cp("pool", pc[:, 0:4], carry[:, ct, 0:4])
                s.act(pc[:, 3:3 + TG], b, AF.Copy)
                yield
                s.ts("dve", cacc, pc[:, 0:TG], convw[:, ct, 0:1], None, ALU.mult)
                for tap in range(1, 4):
                    s.stt(cacc, pc[:, tap:tap + TG], convw[:, ct, tap:tap + 1], cacc, ALU.mult, ALU.add)
                s.cp("pool", carry[:, ct, 0:4], pc[:, TG:TG + 4])
                yield
                s.act(rk, cacc, AF.Tanh, scale=0.5)
                if which == 0:
                    s.stt(qT_, rk, 1.0, cacc, ALU.add, ALU.mult)
                    s.act(sqb, qT_, AF.Square)
                    for j in range(NT):
                        s.mm(bq7[:, j:j + 1], sqb[:, j * 128:(j + 1) * 128], ones_bf[:, 0:1])
                    s.ts("dve", qtmp, bq7[:, 0:NT], 4.0 * EPS, None, ALU.add)
                    s.tt("pool", qss, qtmp, small[:, 1:2].bc([128, NT]), ALU.pow)
                    s.ts("dve", rqs, qss, 128.0 ** -0.5, None, ALU.mult)
                elif which == 1:
                    s.stt(ksil, rk, 1.0, cacc, ALU.add, ALU.mult)
                    s.act(sqb, ksil, AF.Square)
                    for j in range(NT):
                        s.mm(bq7[:, 4 + j:5 + j], sqb[:, j * 128:(j + 1) * 128], ones_bf[:, 0:1])
                    s.ts("dve", qtmp, bq7[:, 4:4 + NT], 4.0 * EPS, None, ALU.add)
                    s.tt("pool", qss, qtmp, small[:, 1:2].bc([128, NT]), ALU.pow)
                    dgk = pcb[0][:, 0:512].re("p (a b) -> p a b", b=128)
                    s.tt("pool", dgk, I4, qss.unsq(2).bc([128, NT, 128]), ALU.mult)
                    bk_ = next_acc()
                    for j in range(NT):
                        s.mm(bk_[:, j * 128:(j + 1) * 128], cm[C_ONES], dgk[:, j, :])
                    s.tt("dve", kT_, ksil, bk_, ALU.mult)
                else:
                    s.stt(vT_, rk, 1.0, cacc, ALU.add, ALU.mult)
                yield
            if g == 0 and h == 0:
                dump("kT", kT_)
                dump("vT", vT_)
            stop_at("dn_proj")
            if hl == 3:
                wzz = load_w([(0, winv[:, :, OFF_DNZ + 512 * hq:OFF_DNZ + 512 * hq + 512])])
                for j in range(NT):
                    b = proj_tok(wzz, j)
                    s.act(zg[:, j, :], b, AF.Tanh, scale=0.5)
                    s.stt(zg[:, j, :], zg[:, j, :], 1.0, b, ALU.add, ALU.mult)
                    yield

        def dn_mat(h):
            hl = h % 4
            hq = h // 4
            par_ = h % 2
            qT_, kT_, vT_ = qT2_[par_], kT2_[par_], vT2_[par_]
            qdT, ke, QKd, wvb, wkT = qdT2[par_], ke2[par_], QKd2[par_], wvb2[par_], wkT2[par_]
            btr = banks[2].bitcast(BF16)
            for j in range(NT):
                s.tr(btr[:, j * 128:(j + 1) * 128], vT_[:, j * 128:(j + 1) * 128], ident_bf)
                s.tr(btr[:, 512 + j * 128:512 + (j + 1) * 128], kT_[:, j * 128:(j + 1) * 128], ident_bf)
            s.act(vtok, btr[:, 0:512].re("p (a b) -> p a b", b=128), AF.Copy, scale=0.5)
            ktr = btr[:, 512:1024].re("p (a b) -> p a b", b=128)
            for j in range(NT):
                s.act(r0k[:, j, :], ktr[:, j, :], AF.Copy, scale=g_egam[:, j, h:h + 1])
                s.act(ke[:, j, :], ktr[:, j, :], AF.Copy, scale=g_kesc[:, j, h:h + 1])
            stop_at("dn_tok")
            yield
            s.tt("pool", dgm, I4, gsum[:, :, 16 + h:17 + h].bc([128, NT, 128]), ALU.mult)
            bA = banks[3]
            bB = banks[4]
            for j in range(NT):
                js = slice(j * 128, (j + 1) * 128)
                s.mm(bA[:, js], cm[C_NEGONES], dgm[:, j, :])
            for j in range(NT):
                js = slice(j * 128, (j + 1) * 128)
                s.mm(bB[:, js], cm[C_NEGONES], dgm[:, j, :], start=True, stop=False)
                s.mm(bB[:, js], cm[C_IDENT], cm[C_NEGMASK], start=False, stop=True)
            yield
            s.act(egbc, bA.re("p (a b) -> p a b", b=128), AF.Exp)
            s.tt("dve", qdT, qT_.re("p (a b) -> p a b", b=128), egbc, ALU.mult)
            for j in range(NT):
                s.act(DmT[:, j, :], bB[:, j * 128:(j + 1) * 128], AF.Exp, bias=gsum[:, j, 16 + h:17 + h])
            s.tt("pool", DmS, DmT, cm[C_STRICT].unsq(1).bc([128, NT, 128]), ALU.mult)
            bK = banks[5]
            bQ = banks[3]
            for j in range(NT):
                js = slice(j * 128, (j + 1) * 128)
                s.mm(bK[:, js], kT_[:, js], kT_[:, js])
                s.mm(bQ[:, js], kT_[:, js], qT_[:, js])
            yield
            N0 = Nm[0]
            s.tt("dve", N0f, bK.re("p (a b) -> p a b", b=128), gbeta[:, :, h:h + 1].bc([128, NT, 128]), ALU.mult)
            s.tt("pool", N0f, N0f, DmS, ALU.mult)
            s.tt("dve", QKd, bQ.re("p (a b) -> p a b", b=128), DmT, ALU.mult)
            yield
            s.cp("act", N0, N0f)
            s.tt("pool", Pm, I4, N0f, ALU.subtract)
            yield
            bT = banks[4].bitcast(BF16)
            for j in range(NT):
                s.tr(bT[:, j * 128:(j + 1) * 128], N0[:, j, :], ident_bf)
            NT0 = NTm[0]
            s.cp("act", NT0, bT[:, 0:512].re("p (a b) -> p a b", b=128))
            s.cp("dve", Pbf, Pm)
            yield
            cur, curT = N0, NT0
            b1 = banks[3]
            b2 = banks[4]
            b3 = banks[5]

            def squares(lev, cur, curT):
                for j in range(NT):
                    js = slice(j * 128, (j + 1) * 128)
                    s.mm(b2[:, js], cur[:, j, :], curT[:, j, :])
                if lev < 4:
                    for j in range(NT):
                        js = slice(j * 128, (j + 1) * 128)
                        s.mm(b1[:, js], curT[:, j, :], cur[:, j, :])
            squares(0, cur, curT)
            yield
            for lev in range(5):
                nxt, nxtT = Nm[(lev + 1) % 2], NTm[(lev + 1) % 2]
                s.cp("act", nxtT, b2.re("p (a b) -> p a b", b=128))
                if lev < 4:
                    s.cp("dve", nxt, b1.re("p (a b) -> p a b", b=128))
                yield
                for j in range(NT):
                    js = slice(j * 128, (j + 1) * 128)
                    s.mm(b3[:, js], nxtT[:, j, :], Pbf[:, j, :])
                if lev < 4:
                    squares(lev + 1, nxt, nxtT)
                yield
                s.tt("dve", Pbf, b3.re("p (a b) -> p a b", b=128), Pm, ALU.add)
                if lev < 4:
                    s.tt("dve", Pm, b3.re("p (a b) -> p a b", b=128), Pm, ALU.add)
                cur, curT = nxt, nxtT
                yield
            if g == 0 and h == 0:
                dump("Pm", Pbf)
            stop_at("dn_neu")
            bW = banks[3]
            bX = banks[4]
            for j in range(NT):
                js = slice(j * 128, (j + 1) * 128)
                s.mm(bW[:, js], Pbf[:, j, :], vtok[:, j, :])
                s.mm(bX[:, js], r0k[:, j, :], Pbf[:, j, :])
            yield
            s.tt("dve", wvb, bW.re("p (a b) -> p a b", b=128), gbeta[:, :, h:h + 1].bc([128, NT, 128]), ALU.mult)
            s.cp("act", wkT, bX.re("p (a b) -> p a b", b=128))
            stop_at("dn_w")
            yield
        def dn_rec(h):
            hl = h % 4
            hq = h // 4
            par_ = h % 2
            qdT, ke, QKd, wvb, wkT, rqs = qdT2[par_], ke2[par_], QKd2[par_], wvb2[par_], wkT2[par_], rqs3[h % 3]
            for j in range(NT):
                for hf in range(2):
                    r = slice(64 * hf, 64 * hf + 64)
                    bU = banks[7][r, 0:128]
                    bO = banks[7][r, 128:256]
                    bS_ = banks[6][:, 0:128]
                    s.mm(bU, wkT[:, j, r], Sbf[:, h, :])
                    yield
                    s.stt(ubf[r, :], bU, gnegbeta[r, j, h:h + 1], wvb[r, j, :], ALU.mult, ALU.add)
                    yield
                    s.mm(bO, qdT[:, j, r], Sbf[:, h, :], start=True, stop=False)
                    s.mm(bO, QKd[:, j, r], ubf, start=False, stop=True)
                    s.mm(bS_, ke[r, j, :], ubf[r, :])
                    yield
                    s.act(obuf[r, j, hl, :], bO, AF.Copy, scale=rqs[r, j:j + 1])
                    s.stt(Sbf[:, h, :], Sst[:, h, :], g_egl[:, j, 16 * hf + h:16 * hf + h + 1], bS_,
                          ALU.mult, ALU.add)
                    s.stt(Sst[:, h, :], Sst[:, h, :], g_egl[:, j, 16 * hf + h:16 * hf + h + 1], bS_,
                          ALU.mult, ALU.add)
                    stop_at("dn_rec1")
                    yield
            if hl == 3:
                if g == 0 and hq == 0:
                    dump("obuf0", obuf)
                for j in range(NT):
                    sqj = pcb[0][:, 0:512].re("p (a b) -> p a b", b=128)
                    s.tt("pool", sqj, obuf[:, j, :, :], obuf[:, j, :, :], ALU.mult)
                    s.reduce_sum(oss[:, j, :], sqj)
                    yield
                rsqrt_(oss, oss, 1.0 / 128.0, otmp)
                s.ts("dve", oss, oss, 0.5, None, ALU.mult)
                s.tt("dve", obuf, obuf, oss.unsq(3).bc([128, NT, 4, 128]), ALU.mult)
                yield
                s.tt("pool", obuf, obuf, dnnw.unsq(1).unsq(1).bc([128, NT, 4, 128]), ALU.mult)
                s.tt("dve", hbb, obuf.re("p a b c -> p a (b c)"), zg, ALU.mult)
                yield
                btr = banks[2].bitcast(BF16)
                for half in range(2):
                    for cc in range(2):
                        cb = half * 2 + cc
                        for j in range(NT):
                            s.tr(btr[:, cc * 512 + j * 128:cc * 512 + (j + 1) * 128],
                                 hbb[:, j, cb * 128:(cb + 1) * 128], ident_bf)
                        s.cp(ev_eng(), hgBT[:, 4 * hq + cb, :], btr[:, cc * 512:(cc + 1) * 512])
                        yield

        run_all(dn_conv(0))
        interleave(dn_mat(0), dn_conv(1), DN_RATIO)
        for h in range(16):
            interleave3(dn_rec(h), dn_mat(h + 1) if h < 15 else None, dn_conv(h + 2) if h < 14 else None)
        if g == 0:
            dump("hgAT", hgAT)
            dump("hgBT", hgBT)

        stop_at("dn")
        A.release(gmark)
        yg = A.alloc([NT, 2048], F32)
        sga = A.alloc([512], F32)
        t1 = [A.alloc([512], F32) for _ in range(2)]
        fss = A.alloc([NT, 2], F32)
        ftmp = A.alloc([NT, 2], F32)
        s.memset("dve", fss, 1.0)
        fjunk = A.alloc([2048], BF16)
        vecbuf = A.alloc([2048], F32)
        s.dma("sp", vecbuf, T(par_d[:, P_FNW:P_FNW + 2048]))
        for j in range(NT):
            s.dma("sp", yg[:, j, :], xg[t0 + j * 128:t0 + (j + 1) * 128, :])
        sga4 = A.alloc([NT, 512], F32)
        for db in range(4):
            for br in range(2):
                plan.append([(0, winv[:, :, (OFF_GA if br == 0 else OFF_GB) + 512 * db:
                                        (OFF_GA if br == 0 else OFF_GB) + 512 * db + 512])])
                plan.append([(0, woutv[:, 16 * br:16 * br + 16, db * 512:(db + 1) * 512])])
        for db in range(4):
            dsl = slice(db * 512, (db + 1) * 512)
            for br in range(2):
                hsrc = hgAT if br == 0 else hgBT
                wgt = get_w()
                for j in range(NT):
                    b = proj_tok(wgt, j)
                    s.act(sga4[:, j, :], b, AF.Tanh, scale=0.5)
                wot = get_w()
                for j in range(NT):
                    b2 = proj_tok(wot, j, src=hsrc)
                    tt_ = t1[j % 2]
                    s.stt(tt_, sga4[:, j, :], 1.0, b2, ALU.add, ALU.mult)
                    s.stt(yg[:, j, dsl], tt_, 0.5, yg[:, j, dsl], ALU.mult, ALU.add)
        for j in range(NT):
            s.act(fjunk, yg[:, j, :], AF.Square, accum_out=fss[:, j, 0:1])
            rsqrt_(fss[:, j, :], fss[:, j, :], 1.0 / D_MODEL, ftmp[:, j, :])
            s.stt(yg[:, j, :], yg[:, j, :], fss[:, j, 0:1], vecbuf, ALU.mult, ALU.mult)
            s.out_dmas.append(s.dma("sp", outg[t0 + j * 128:t0 + (j + 1) * 128, :], yg[:, j, :]))
        stop_at("group0")

    try:
        for g in range(NG):
            body(g)
    except _Stop:
        pass
    s.emit()
    return nc


def make_params(inputs):
    f = np.float32
    p = np.zeros((128, NPARAM), f)
    p[:, P_NW:P_NW + 2048] = inputs["norm_w"][0][None, :]
    p[:, P_FNW:P_FNW + 2048] = inputs["final_norm_w"][None, :]
    p[:, P_MLNW:P_MLNW + 16] = inputs["ml_norm_w"][0].reshape(16, 128).T
    p[:, P_DNNW:P_DNNW + 128] = inputs["dn_norm_w"][0][None, :]
    p[:, P_GBIAS:P_GBIAS + 8] = inputs["ml_i_bias"][0][None, :]
    p[:, P_GBIAS + 8:P_GBIAS + 16] = inputs["ml_f_bias"][0][None, :]
    p[:, P_GBIAS + 32:P_GBIAS + 48] = inputs["dn_dt_bias"][0][None, :]
    p[:, P_ALOG:P_ALOG + 16] = inputs["dn_a_log"][0][None, :]
    cw = inputs["dn_conv_w"][0]
    p[:, P_CONV:P_CONV + 192] = cw.reshape(4, 48, 128).transpose(2, 1, 0).reshape(128, 192)
    return p


_NC_CACHE = {}


def kernel(**inputs):
    inputs = {k: np.asarray(v) for k, v in inputs.items()}
    x = np.ascontiguousarray(inputs["x"], dtype=np.float32)
    w_in = np.ascontiguousarray(inputs["w_in"][0], dtype=np.float32)
    w_out = np.ascontiguousarray(inputs["w_out"][0], dtype=np.float32)
    params = make_params(inputs)
    consts = host_consts()
    if "nc" not in _NC_CACHE:
        _NC_CACHE["nc"] = build()
    nc = _NC_CACHE["nc"]
    n = x.shape[0]
    in_maps = [{"x": x[b], "w_in": w_in, "w_out": w_out, "params": params, "consts": consts}
               for b in range(n)]
    res = run_bass_kernel_spmd(nc, in_maps, core_ids=list(range(n)))
    return np.stack([r["out"] for r in res.results], axis=0).astype(np.float32)
```
